# Optimizing a Trainium2 kernel written in Bass

```python
import jax, jax.numpy as jnp
from jax import lax
import numpy as np

D_MODEL = 1024
BATCH = 4
SEQ = 4096
DEPTH = 1
DEC_BATCH = 32
DEC_SEQ = 32
PAST_LEN = 4096

CHUNK = 64
Q_BLOCK = 128
MIX_W = D_MODEL
LRU_W = MIX_W // 2
LRU_HEADS = 8
LRU_BLK = LRU_W // LRU_HEADS
CONV_W = 4
LRU_C = 8.0
MLA_HEADS = 8
QK_NOPE = 64
QK_ROPE = 32
V_DIM = 64
Q_LORA = D_MODEL // 4
KV_LORA = D_MODEL // 8
MLA_W = MLA_HEADS * V_DIM
IN_COLS = 2 * LRU_W + Q_LORA + KV_LORA + QK_ROPE
D_FF = 4 * D_MODEL
ROPE_BASE = 10000.0
EPS = 1e-6
ATTN_SCALE = (QK_NOPE + QK_ROPE) ** -0.5
NEG_INF = -1e30

kernel_name = "hymba_rglru_mla_streaming_step"


def rms_norm(x, g):
    xf = x.astype(jnp.float32)
    y = xf * lax.rsqrt(jnp.mean(xf * xf, axis=-1, keepdims=True) + EPS)
    return (y * g.astype(jnp.float32)).astype(x.dtype)


def rope_tables(pos):
    inv = ROPE_BASE ** (-jnp.arange(0, QK_ROPE, 2, dtype=jnp.float32) / QK_ROPE)
    ang = pos.astype(jnp.float32)[:, None] * inv[None, :]
    return jnp.cos(ang), jnp.sin(ang)


def apply_rope(x, cos, sin):
    xf = x.astype(jnp.float32)
    x1, x2 = jnp.split(xf, 2, axis=-1)
    out = jnp.concatenate([x1 * cos - x2 * sin, x2 * cos + x1 * sin], axis=-1)
    return out.astype(x.dtype)


def causal_conv(xb, buf, w, b):
    T = xb.shape[1]
    xp = jnp.concatenate([buf, xb], axis=1)
    y = b + sum(xp[:, k:k + T] * w[k] for k in range(CONV_W))
    return y, xp[:, -(CONV_W - 1):]


def rglru(xc, h0, w_a, b_a, w_i, b_i, lam):
    B, T, C = xc.shape
    xr = xc.reshape(B, T, LRU_HEADS, LRU_BLK)
    r = jax.nn.sigmoid(jnp.einsum('btnc,ncd->btnd', xr, w_a).reshape(B, T, C) + b_a)
    i = jax.nn.sigmoid(jnp.einsum('btnc,ncd->btnd', xr, w_i).reshape(B, T, C) + b_i)
    log_a = -LRU_C * r.astype(jnp.float32) * jax.nn.softplus(-lam.astype(jnp.float32))
    a = jnp.exp(log_a)
    bx = jnp.sqrt(-jnp.expm1(2.0 * log_a)) * (i * xc).astype(jnp.float32)

    def combine(lhs, rhs):
        a1, b1 = lhs
        a2, b2 = rhs
        return a1 * a2, a2 * b1 + b2

    a_cum, b_cum = lax.associative_scan(combine, (a, bx), axis=1)
    h = a_cum * h0.astype(jnp.float32)[:, None, :] + b_cum
    return h.astype(xc.dtype), h[:, -1].astype(h0.dtype)


def mla_attend(q_lat, q_rope, ckv, krope, mask):
    s = (jnp.einsum('bqhr,bkr->bhqk', q_lat, ckv).astype(jnp.float32)
         + jnp.einsum('bqhe,bke->bhqk', q_rope, krope).astype(jnp.float32)) * ATTN_SCALE
    if mask is not None:
        s = jnp.where(mask[None, None], s, NEG_INF)
    p = jax.nn.softmax(s, axis=-1).astype(ckv.dtype)
    return jnp.einsum('bhqk,bkr->bqhr', p, ckv)


def mla_prompt(q_lat, q_rope, ckv, krope):
    B, T, H, R = q_lat.shape
    n_blk = T // Q_BLOCK
    k_chunk = jnp.arange(ckv.shape[1]) // CHUNK

    def one_block(blk):
        start = blk * Q_BLOCK
        qs = lax.dynamic_slice_in_dim(q_lat, start, Q_BLOCK, axis=1)
        qr = lax.dynamic_slice_in_dim(q_rope, start, Q_BLOCK, axis=1)
        q_chunk = (start + jnp.arange(Q_BLOCK)) // CHUNK
        mask = k_chunk[None, :] <= q_chunk[:, None]
        return mla_attend(qs, qr, ckv, krope, mask)

    o = lax.map(one_block, jnp.arange(n_blk))
    return jnp.transpose(o, (1, 0, 2, 3, 4)).reshape(B, T, H, R)


def trunk_layer(x, pos, ckv_past, krope_past, h0, conv0, lw):
    (g_pre_mix, g_post_mix, g_pre_mlp, g_post_mlp, w_in, conv_w, conv_b,
     w_a, b_a, w_i, b_i, lam, g_q, w_uq, g_kv, w_ukv, w_out, w_up, w_down) = lw
    B, T, _ = x.shape
    xn = rms_norm(x, g_pre_mix)
    proj = jnp.einsum('btd,dc->btc', xn, w_in)
    x_lru, gate, c_q, c_kv, k_r = jnp.split(
        proj, [LRU_W, 2 * LRU_W, 2 * LRU_W + Q_LORA, 2 * LRU_W + Q_LORA + KV_LORA], axis=-1)

    xc, conv_new = causal_conv(x_lru, conv0, conv_w, conv_b)
    h, h_last = rglru(xc, h0, w_a, b_a, w_i, b_i, lam)
    y_lru = h * jax.nn.gelu(gate)

    cos, sin = rope_tables(pos)
    q = jnp.einsum('btr,rhe->bthe', rms_norm(c_q, g_q), w_uq)
    q_nope = q[..., :QK_NOPE]
    q_rope = apply_rope(q[..., QK_NOPE:], cos[:, None, :], sin[:, None, :])
    ckv_new = rms_norm(c_kv, g_kv)
    krope_new = apply_rope(k_r, cos, sin)
    w_uk = w_ukv[..., :QK_NOPE]
    w_uv = w_ukv[..., QK_NOPE:]
    q_lat = jnp.einsum('bthn,rhn->bthr', q_nope, w_uk)
    if ckv_past is None:
        o_lat = mla_prompt(q_lat, q_rope, ckv_new, krope_new)
    else:
        ckv_all = jnp.concatenate([ckv_past, ckv_new], axis=1)
        krope_all = jnp.concatenate([krope_past, krope_new], axis=1)
        o_lat = mla_attend(q_lat, q_rope, ckv_all, krope_all, None)
    o_mla = jnp.einsum('bthr,rhv->bthv', o_lat, w_uv).reshape(B, T, MLA_W)

    mix = jnp.einsum('btc,cd->btd', jnp.concatenate([y_lru, o_mla], axis=-1), w_out)
    x = x + rms_norm(mix, g_post_mix)

    hid = jax.nn.relu(jnp.einsum('btd,df->btf', rms_norm(x, g_pre_mlp), w_up))
    ff = jnp.einsum('btf,fd->btd', jnp.square(hid), w_down)
    x = x + rms_norm(ff, g_post_mlp)
    return x, ckv_new, krope_new, h_last, conv_new


def setup_inputs(seed: int = 0) -> dict:
    key = jax.random.key(seed)
    ks = jax.random.split(key, 32)
    f32 = jnp.float32

    def nrm(k, shape, scale):
        return jax.random.normal(k, shape, f32) * scale

    def gain(k, n):
        return 1.0 + 0.05 * jax.random.normal(k, (DEPTH, n), f32)

    a0 = jax.random.uniform(ks[31], (DEPTH, LRU_W), f32, 0.9, 0.999)
    return {
        "x_prompt": nrm(ks[0], (BATCH, SEQ, D_MODEL), 1.0),
        "x_sample": nrm(ks[1], (DEC_BATCH, DEC_SEQ, D_MODEL), 1.0),
        "cache_ckv": nrm(ks[2], (DEPTH, DEC_BATCH, PAST_LEN, KV_LORA), 1.0),
        "cache_krope": nrm(ks[3], (DEPTH, DEC_BATCH, PAST_LEN, QK_ROPE), 0.5),
        "state_lru_h": nrm(ks[4], (DEPTH, DEC_BATCH, LRU_W), 0.5),
        "state_conv": nrm(ks[5], (DEPTH, DEC_BATCH, CONV_W - 1, LRU_W), 0.5),
        "norm_pre_mix": gain(ks[6], D_MODEL),
        "norm_post_mix": gain(ks[7], D_MODEL),
        "norm_pre_mlp": gain(ks[8], D_MODEL),
        "norm_post_mlp": gain(ks[9], D_MODEL),
        "w_in": nrm(ks[10], (DEPTH, D_MODEL, IN_COLS), D_MODEL ** -0.5),
        "conv_w": nrm(ks[11], (DEPTH, CONV_W, LRU_W), 0.5),
        "conv_b": nrm(ks[12], (DEPTH, LRU_W), 0.02),
        "lru_w_a": nrm(ks[13], (DEPTH, LRU_HEADS, LRU_BLK, LRU_BLK), LRU_BLK ** -0.5),
        "lru_b_a": nrm(ks[14], (DEPTH, LRU_W), 0.02),
        "lru_w_i": nrm(ks[15], (DEPTH, LRU_HEADS, LRU_BLK, LRU_BLK), LRU_BLK ** -0.5),
        "lru_b_i": nrm(ks[16], (DEPTH, LRU_W), 0.02),
        "lru_lambda": jnp.log(a0) - jnp.log1p(-a0),
        "q_norm": gain(ks[17], Q_LORA),
        "w_uq": nrm(ks[18], (DEPTH, Q_LORA, MLA_HEADS, QK_NOPE + QK_ROPE), Q_LORA ** -0.5),
        "kv_norm": gain(ks[19], KV_LORA),
        "w_ukv": nrm(ks[20], (DEPTH, KV_LORA, MLA_HEADS, QK_NOPE + V_DIM), KV_LORA ** -0.5),
        "w_out": nrm(ks[21], (DEPTH, MIX_W, D_MODEL), MIX_W ** -0.5),
        "w_up": nrm(ks[22], (DEPTH, D_MODEL, D_FF), D_MODEL ** -0.5),
        "w_down": nrm(ks[23], (DEPTH, D_FF, D_MODEL), D_FF ** -0.5),
    }


def reference(x_prompt, x_sample, cache_ckv, cache_krope, state_lru_h, state_conv,
              norm_pre_mix, norm_post_mix, norm_pre_mlp, norm_post_mlp, w_in, conv_w, conv_b,
              lru_w_a, lru_b_a, lru_w_i, lru_b_i, lru_lambda, q_norm, w_uq, kv_norm, w_ukv,
              w_out, w_up, w_down):
    B, T_p, _ = x_prompt.shape
    T_s = x_sample.shape[1]
    pos_p = jnp.arange(T_p)
    pos_s = PAST_LEN + jnp.arange(T_s)
    h0_p = jnp.zeros((B, LRU_W), x_prompt.dtype)
    conv0_p = jnp.zeros((B, CONV_W - 1, LRU_W), x_prompt.dtype)

    y_p, y_s = x_prompt, x_sample
    st_p, st_s = [], []
    for l in range(DEPTH):
        lw = (norm_pre_mix[l], norm_post_mix[l], norm_pre_mlp[l], norm_post_mlp[l],
              w_in[l], conv_w[l], conv_b[l], lru_w_a[l], lru_b_a[l], lru_w_i[l], lru_b_i[l],
              lru_lambda[l], q_norm[l], w_uq[l], kv_norm[l], w_ukv[l], w_out[l], w_up[l], w_down[l])
        y_p, ckv_p, kr_p, h_p, cv_p = trunk_layer(y_p, pos_p, None, None, h0_p, conv0_p, lw)
        y_s, ckv_s, kr_s, h_s, cv_s = trunk_layer(y_s, pos_s, cache_ckv[l], cache_krope[l],
                                                  state_lru_h[l], state_conv[l], lw)
        st_p.append((ckv_p, kr_p, h_p, cv_p))
        st_s.append((ckv_s, kr_s, h_s, cv_s))

    new_ckv_p, new_kr_p, new_h_p, new_cv_p = [jnp.stack(a) for a in zip(*st_p)]
    new_ckv_s, new_kr_s, new_h_s, new_cv_s = [jnp.stack(a) for a in zip(*st_s)]
    return (y_p, y_s, new_ckv_p, new_kr_p, new_h_p, new_cv_p,
            new_ckv_s, new_kr_s, new_h_s, new_cv_s)
```

```python
import numpy as np
import concourse.bass as bass
import concourse.mybir as mybir
from concourse.bass_utils import run_bass_kernel_spmd

F32 = mybir.dt.float32
BF16 = mybir.dt.bfloat16
AF = mybir.ActivationFunctionType
ALU = mybir.AluOpType

NCORES = 8
D = 1024
NIN = 1440
SEQ = 4096
NB = 32
NOWN = 16
H = 8
EPS = 1e-6
ATTN_SCALE = 96.0 ** -0.5
NEG = -30000.0
GELU_C = 1.5957691216057308
STAGE = None
SAME_ENG_DIST = 8
BACKGROUND = True
DEFER_TAIL = True
LACC_POOL = False
SAFE_MERGE = True
CROSS_STILE = True
SAMPLE_PREP_BG = True
TIMED_MERGE = False


class _Stop(Exception):
    pass


class _Op:
    __slots__ = ("eng", "fn", "idx", "dma", "deps", "inc", "dsem", "dtarget", "waits")


class Sched:
    ENGS = ("pe", "act", "dve", "pool", "sp")

    def __init__(self, n_dma_sems):
        self.ops = {e: [] for e in self.ENGS}
        self.last_w = {}
        self.readers = {}
        self.n_dma = n_dma_sems
        self.dma_state = {e: {"next": 0, "last": [None] * n_dma_sems[e], "cnt": [0] * n_dma_sems[e]}
                          for e in n_dma_sems}
        self.all_dma = []
        self.defer = None

    def op(self, eng, fn, reads=(), writes=(), dma=False, grp=None, atom=None, cost=0.3):
        if self.defer is not None:
            item = (eng, fn, list(reads), list(writes), dma, grp, cost)
            if atom == "begin":
                self.defer.append([item])
            elif atom in ("mid", "end"):
                self.defer[-1].append(item)
            else:
                self.defer.append(item)
            return None
        return self._op(eng, fn, reads, writes, dma, grp)

    def flush(self, lst, n):
        k = 0
        while lst and k < n:
            it = lst.pop(0)
            if isinstance(it, list):
                for sub in it:
                    self._op(*sub[:6])
            else:
                self._op(*it[:6])
            k += 1

    def _op(self, eng, fn, reads=(), writes=(), dma=False, grp=None):
        o = _Op()
        o.eng, o.fn, o.dma, o.inc = eng, fn, dma, False
        o.idx = len(self.ops[eng])
        deps = {}
        for k in reads:
            w = self.last_w.get(k)
            if w is not None:
                deps[(w.eng, w.idx)] = (w, True)
        for k in writes:
            w = self.last_w.get(k)
            if w is not None and (w.eng, w.idx) not in deps:
                deps[(w.eng, w.idx)] = (w, False)
            for r in self.readers.get(k, ()):
                if (r.eng, r.idx) not in deps:
                    deps[(r.eng, r.idx)] = (r, False)
        if dma:
            q = grp if grp is not None else eng
            st = self.dma_state[q]
            slot = st["next"] % self.n_dma[q]
            st["next"] += 1
            prev = st["last"][slot]
            if prev is not None:
                deps[(prev.eng, prev.idx)] = (prev, True)
            st["last"][slot] = o
            st["cnt"][slot] += 1
            o.dsem = (q, slot)
            o.dtarget = 16 * st["cnt"][slot]
            self.all_dma.append(o)
        o.deps = list(deps.values())
        for k in reads:
            self.readers.setdefault(k, []).append(o)
        for k in writes:
            self.last_w[k] = o
            self.readers[k] = []
        self.ops[eng].append(o)
        return o

    def finalize(self):
        for e in self.ENGS:
            for o in self.ops[e]:
                o.waits = []
                for (d, raw) in o.deps:
                    if d.dma:
                        o.waits.append(("dma", d))
                    elif d.eng != o.eng:
                        d.inc = True
                        o.waits.append(("eng", d))
                    else:
                        if (o.idx - d.idx <= SAME_ENG_DIST and o.eng != "pe") or o.dma or o.eng == "pool":
                            d.inc = True
                            o.waits.append(("eng", d))
        self.cum = {}
        for e in self.ENGS:
            c = 0
            arr = []
            for o in self.ops[e]:
                if o.inc and not o.dma:
                    c += 1
                arr.append(c)
            self.cum[e] = arr

    def emit(self, eng_name, engine, esems, dsems):
        maxw = {}
        for o in self.ops[eng_name]:
            for kind, d in o.waits:
                if kind == "dma":
                    key = ("dma",) + d.dsem
                    val = d.dtarget
                    sem = dsems[d.dsem[0]][d.dsem[1]]
                else:
                    key = ("eng", d.eng)
                    val = self.cum[d.eng][d.idx]
                    sem = esems[d.eng]
                if maxw.get(key, 0) >= val:
                    continue
                maxw[key] = val
                engine.wait_ge(sem, val)
            ins = o.fn(engine)
            if o.dma:
                ins.then_inc(dsems[o.dsem[0]][o.dsem[1]], 16)
            elif o.inc:
                ins.then_inc(esems[eng_name], 1)


def build_nc():
    nc = bass.Bass("TRN2", target_bir_lowering=False)
    S = Sched({"sp": 16, "pool": 8, "scr": 16})

    def din(name, shape, dt=F32):
        return nc.dram_tensor(name, list(shape), dt, kind="ExternalInput").ap()

    def dout(name, shape, dt=F32):
        return nc.dram_tensor(name, list(shape), dt, kind="ExternalOutput").ap()

    xp_all = din("xp_all", [SEQ, D])
    xp_own = din("xp_own", [NOWN * 128, D])
    xs_tok = din("xs_tok", [128, D])
    cckv = din("cckv", [4, SEQ, 128])
    ckr = din("ckr", [4, SEQ, 32])
    st_h = din("st_h", [128, 4, 4])
    st_conv = din("st_conv", [128, 4, 4, 3])
    w_in = din("w_in", [D, NIN])
    w_out = din("w_out", [D, D])
    w_up = din("w_up", [D, 4096])
    w_down = din("w_down", [4096, D])
    w_uq = din("w_uq", [256, 768])
    w_ukv = din("w_ukv", [128, 1024])
    w_a = din("w_a", [8, 64, 64])
    w_i = din("w_i", [8, 64, 64])
    vec = din("vec", [128, 4, 8])
    gpm = din("gpm", [128, 8])
    g_post_mix = din("g_post_mix", [128, D])
    g_pre_mlp = din("g_pre_mlp", [128, D])
    g_post_mlp = din("g_post_mlp", [128, D])
    g_q = din("g_q", [128, 256])
    g_kv = din("g_kv", [128, 128])
    ident_d = din("ident", [128, 128])
    maskb_d = din("maskb", [128, 2, 512])
    masks_d = din("masks", [128, 4, 256])
    sel_d = din("sel", [128, 2])
    tab_real = din("tab_real", [NB + 1, 128, 32])
    tab_own = din("tab_own", [NOWN, 128, 32])

    wup_bf = nc.dram_tensor("wup_bf", [D, 4096], BF16, kind="Internal").ap()
    wdn_bf = nc.dram_tensor("wdn_bf", [4096, D], BF16, kind="Internal").ap()

    y_own = dout("y_own", [NOWN * 128, D])
    y_s = dout("y_s", [128, D])
    ckv_p = dout("ckv_p", [SEQ, 128])
    kr_p = dout("kr_p", [SEQ, 32])
    h_p = dout("h_p", [128, 4])
    conv_p = dout("conv_p", [128, 4, 3])
    ckv_s = dout("ckv_s", [128, 128])
    kr_s = dout("kr_s", [128, 32])
    h_s = dout("h_s", [128, 4, 4])
    conv_s = dout("conv_s", [128, 4, 4, 3])

    sb_ptr = [(nc.sbuf_base + 63) // 64 * 64]
    sb_top = nc.sbuf_top

    def sb(name, shape, dt, at=None):
        nbytes = int(np.prod(shape[1:])) * (4 if dt == F32 else 2)
        nbytes = (nbytes + 63) // 64 * 64
        if at is None:
            off = sb_ptr[0]
            sb_ptr[0] += nbytes
            assert sb_ptr[0] <= sb_top, f"SBUF overflow at {name}: {sb_ptr[0]} > {sb_top}"
        else:
            off = at[0]
            at[0] += nbytes
            assert at[0] <= at[1], f"arena overflow at {name}: {at[0]} > {at[1]}"
        return nc.alloc_sbuf_tensor_at(name, list(shape), dt, offset=off)

    w_in_bf = sb("w_in_bf", [128, 8, NIN], BF16)
    w_out_lru = sb("w_out_lru", [128, 4, D], BF16)
    W_comb = sb("W_comb", [128, 8, D], BF16)
    W_ql = sb("W_ql", [128, 2, 8, 128], BF16)
    W_qr = sb("W_qr", [128, 2, 8, 32], BF16)
    Wa_bd = sb("Wa_bd", [128, 4, 128], BF16)
    Wi_bd = sb("Wi_bd", [128, 4, 128], BF16)
    gpostmix = sb("gpostmix", [128, D], F32)
    gpremlp = sb("gpremlp", [128, D], F32)
    gpostmlp = sb("gpostmlp", [128, D], F32)
    gq = sb("gq", [128, 256], F32)
    gkv = sb("gkv", [128, 128], F32)
    ident_bf = sb("ident_bf", [128, 128], BF16)
    ident_f = sb("ident_f", [128, 128], F32)
    ones_bf = sb("ones_bf", [128, 128], BF16)
    ones_f = sb("ones_f", [128, 128], F32)
    maskb = sb("maskb", [128, 2, 512], BF16)
    masks = sb("masks", [128, 4, 256], BF16)
    vecs = sb("vecs", [128, 4, 8], F32)
    sp8 = sb("sp8", [128, 4], F32)
    gpm_sb = sb("gpm_sb", [128, 8], F32)
    sel = sb("sel", [128, 2], F32)
    onecol = sb("onecol", [128, 1], F32)
    epscol = sb("epscol", [128, 1], F32)
    st_h_sb = sb("st_h_sb", [128, 4, 4], F32)
    hs_sb = sb("hs_sb", [128, 4, 4], F32)
    KTn = sb("KTn", [128, 128], BF16)
    krTn = sb("krTn", [32, 128], BF16)
    Vn = sb("Vn", [128, 128], BF16)
    KT = sb("KT", [128, SEQ], BF16)
    krT = sb("krT", [32, SEQ], BF16)
    V = sb("V", [128, NB, 128], BF16)
    x1 = sb("x1", [128, 4, D], F32)
    xn2T = sb("xn2T", [128, 8, 512], BF16)
    NWS = 2
    wup = [sb(f"wup{i}", [128, 8, 256], BF16) for i in range(NWS)]
    wdn_off = []
    wdn = []
    for i in range(NWS):
        wdn_off.append(sb_ptr[0])
        wdn.append(sb(f"wdn{i}", [128, 4, 512], BF16))
    qlatT2 = nc.alloc_sbuf_tensor_at("qlatT2", [128, 8, 128], BF16, offset=wdn_off[0])
    y_lruT2 = nc.alloc_sbuf_tensor_at("y_lruT2", [128, 4, 128], BF16, offset=wdn_off[0] + 2048)
    qrT2 = nc.alloc_sbuf_tensor_at("qrT2", [32, 8, 128], BF16, offset=wdn_off[1])
    xl_p_off = sb_ptr[0]
    xl_p = sb("xl_p", [128, 4, 3 + 256], F32)
    hstate = sb("hstate", [128, 4], F32)
    stat = sb("stat", [128, 64], F32)

    arena0 = sb_ptr[0]
    arena_end = sb_top

    aM = [arena0, arena_end]
    x_in0_off = aM[0]
    x_in = [sb(f"x_in{i}", [128, D], F32, aM) for i in range(2)]
    x_in1_off = x_in0_off + 4096
    xs = sb("xs", [128, D], BF16, aM)
    xnT = sb("xnT", [128, 8, 256], BF16, aM)
    xnT_own = sb("xnT_own", [128, 8, 128], BF16, aM)
    kv_f = sb("kv_f", [128, 160], F32, aM)
    kv_bf = sb("kv_bf", [128, 160], BF16, aM)
    tabr = [sb(f"tabr{i}", [128, 32], F32, aM) for i in range(2)]
    tabo = sb("tabo", [128, 32], F32, aM)
    rtmp = sb("rtmp", [128, 4, 16], F32, aM)
    xc = sb("xc", [128, 4, 256], F32, aM)
    xc_bf = sb("xc_bf", [128, 4, 256], BF16, aM)
    t_r = sb("t_r", [128, 4, 256], F32, aM)
    t_i = sb("t_i", [128, 4, 256], F32, aM)
    t_s = sb("t_s", [128, 256], F32, aM)
    hT = sb("hT", [128, 4, 256], F32, aM)
    hsel = sb("hsel", [128, 4, 128], F32, aM)
    g_a = sb("g_a", [128, 4, 128], F32, aM)
    g_b = sb("g_b", [128, 4, 128], F32, aM)
    cqn = sb("cqn", [128, 256], BF16, aM)
    cqnT = sb("cqnT", [128, 2, 128], BF16, aM)
    qr_r = sb("qr_r", [128, 8, 32], F32, aM)
    qr_s = sb("qr_s", [128, 8, 32], F32, aM)
    qr_bf = sb("qr_bf", [128, 8, 32], BF16, aM)
    PT = [sb(f"PT{i}", [128, 512], BF16, aM) for i in range(3)]
    Lacc = [sb(f"Lacc{i}", [128, 512], F32, aM) for i in range(2)]
    olatT = sb("olatT", [128, 8, 128], BF16, aM)
    mix_sb = sb("mix_sb", [128, D], F32, aM)
    xs2 = xs
    krc = nc.alloc_sbuf_tensor_at("krc", [128, NB, 32], BF16, offset=x_in1_off)
    xl_s = nc.alloc_sbuf_tensor_at("xl_s", [128, 4, 4, 35], F32, offset=xl_p_off)
    surv_off = aM[0]
    y_lruT = sb("y_lruT", [128, 4, 128], BF16, aM)
    qlatT = sb("qlatT", [128, 8, 128], BF16, aM)
    qrT = sb("qrT", [32, 8, 128], BF16, aM)
    aS = [arena0, arena_end]
    stg_in = [sb(f"stg_in{i}", [128, NIN], F32, aS) for i in range(2)]
    uq_f = sb("uq_f", [128, 2, 768], F32, aS)
    ukv_f = sb("ukv_f", [128, 1024], F32, aS)
    uqT = sb("uqT", [64, 2, 8, 128], BF16, aS)
    ukT = sb("ukT", [64, 8, 128], BF16, aS)
    uvT = sb("uvT", [64, 8, 128], BF16, aS)
    wo_mla = sb("wo_mla", [64, 8, D], BF16, aS)
    lam_e = sb("lam_e", [128, 4], F32, aS)
    aF = [arena0, arena_end]
    hid = sb("hid", [128, 32, 512], BF16, aF)
    ff_sb = sb("ff_sb", [128, 4, D], F32, aF)
    rl = [sb(f"rl{i}", [128, 512], F32, aF) for i in range(2)]
    assert aF[0] <= surv_off, f"F arena {aF[0]} overlaps survivors at {surv_off}"
    assert aS[0] <= surv_off

    banks = [nc.alloc_psum_tensor(f"bank{i}", [128, 512], F32) for i in range(8)]
    banks_bf = [b.bitcast(BF16) for b in banks]

    def BK(b, lo=0, hi=512):
        return [("bk", b)]

    def stage(k):
        if STAGE is not None and STAGE == k:
            raise _Stop()

    klog = {"keys": set()}

    def R(*ks):
        out = ["PHASE"]
        for k in ks:
            if isinstance(k, list):
                out.extend(k)
                klog["keys"].update(k)
            else:
                out.append(k)
                klog["keys"].add(k)
        return out

    def W(*ks):
        out = []
        for k in ks:
            if isinstance(k, list):
                out.extend(k)
                klog["keys"].update(k)
            else:
                out.append(k)
                klog["keys"].add(k)
        return out

    def phase_switch():
        keys = list(klog["keys"])
        S.op("dve", lambda e: e.memset(stat[:, 63:64], 0.0), reads=[], writes=keys + ["PHASE"])
        klog["keys"] = set()

    def fsz(ap):
        try:
            return int(ap.free_size)
        except Exception:
            return 256

    def dma(out_ap, in_ap, reads, writes, eng="sp", grp=None):
        return S.op(eng, lambda e: e.dma_start(out=out_ap, in_=in_ap), reads=reads, writes=writes, dma=True, grp=grp,
                    cost=2.5)

    def ecost(eng, out_ap):
        n_ = fsz(out_ap)
        if eng == "act":
            return 0.25 + n_ / 1400.0
        if eng == "pool":
            return 0.15 + n_ / 450.0
        return 0.08 + n_ / 960.0

    def mm(out_ap, lhsT, rhs, start, stop, reads, writes, noatom=False):
        atom = None
        if noatom:
            pass
        elif start and not stop:
            atom = "begin"
        elif not start:
            atom = "end" if stop else "mid"
        return S.op("pe", lambda e: e.matmul(out_ap, lhsT, rhs, start=start, stop=stop), reads=reads, writes=writes,
                    atom=atom, cost=max(0.06, fsz(out_ap) / 1200.0))

    def tr(out_ap, in_ap, ident_ap, reads, writes):
        return S.op("pe", lambda e: e.transpose(out_ap, in_ap, ident_ap), reads=reads, writes=writes, cost=0.13)

    def act(out_ap, in_ap, func, reads, writes, bias=None, scale=None, accum_out=None):
        kw = {}
        if bias is not None:
            kw["bias"] = bias
        if scale is not None:
            kw["scale"] = scale
        if accum_out is not None:
            kw["accum_out"] = accum_out
        return S.op("act", lambda e: e.activation(out_ap, in_ap, func, **kw), reads=reads, writes=writes,
                    cost=0.25 + fsz(out_ap) / 1400.0)

    def ts(eng, out_ap, in0, s1, s2, op0, op1, reads, writes):
        c_ = ecost(eng, out_ap)
        if op1 is None:
            return S.op(eng, lambda e: e.tensor_scalar(out_ap, in0, s1, None, op0), reads=reads, writes=writes, cost=c_)
        return S.op(eng, lambda e: e.tensor_scalar(out_ap, in0, s1, s2, op0, op1), reads=reads, writes=writes, cost=c_)

    def tt(eng, out_ap, in0, in1, op, reads, writes):
        return S.op(eng, lambda e: e.tensor_tensor(out_ap, in0, in1, op), reads=reads, writes=writes, cost=ecost(eng, out_ap))

    def stt(out_ap, in0, scalar, in1, op0, op1, reads, writes):
        return S.op("dve", lambda e: e.scalar_tensor_tensor(out_ap, in0, scalar, in1, op0, op1),
                    reads=reads, writes=writes, cost=ecost("dve", out_ap))

    def cp(eng, out_ap, in_ap, reads, writes):
        if eng == "act":
            return S.op("act", lambda e: e.copy(out_ap, in_ap), reads=reads, writes=writes, cost=ecost("act", out_ap))
        return S.op(eng, lambda e: e.tensor_copy(out_ap, in_ap), reads=reads, writes=writes, cost=ecost(eng, out_ap))

    scol = [0]

    def newstat(n=1):
        c = scol[0]
        if c + n > 60:
            c = 0
        scol[0] = c + n
        return c

    def rstd_from(cin, n, cout):
        act(stat[:, cout:cout + 1], stat[:, cin:cin + 1], AF.Ln, reads=R(("stat", cin), "epscol"), writes=W(("stat", cout)),
            scale=1.0 / n, bias=epscol[:, 0:1])
        act(stat[:, cout:cout + 1], stat[:, cout:cout + 1], AF.Exp, reads=R(("stat", cout)), writes=W(("stat", cout)),
            scale=-0.5)

    dma(vecs[:], vec, R(), W("vecs"))
    dma(gpm_sb[:], gpm, R(), W("gpm"))
    dma(sel[:], sel_d, R(), W("sel"))
    dma(gpostmix[:], g_post_mix, R(), W("gpostmix"))
    dma(gpremlp[:], g_pre_mlp, R(), W("gpremlp"))
    dma(gpostmlp[:], g_post_mlp, R(), W("gpostmlp"))
    dma(gq[:], g_q, R(), W("gq"))
    dma(gkv[:], g_kv, R(), W("gkv"))
    dma(ident_f[:], ident_d, R(), W("ident_f"))
    dma(ident_bf[:], ident_d, R(), W("ident_bf"), eng="pool")
    dma(maskb[:], maskb_d, R(), W("maskb"), eng="pool")
    dma(masks[:], masks_d, R(), W("masks"), eng="pool")
    S.op("dve", lambda e: e.memset(ones_bf[:], 1.0), reads=R(), writes=W("ones"))
    S.op("dve", lambda e: e.memset(ones_f[:], 1.0), reads=R(), writes=W("ones_f"))
    S.op("dve", lambda e: e.memset(onecol[:], 1.0), reads=R(), writes=W("onecol"))
    S.op("dve", lambda e: e.memset(epscol[:], EPS), reads=R(), writes=W("epscol"))
    S.op("dve", lambda e: e.memset(hstate[:], 0.0), reads=R(), writes=W(*[("hstate", ct) for ct in range(4)]))
    S.op("pool", lambda e: e.memset(Wa_bd[:], 0.0), reads=R(), writes=W("Wa_bd"))
    S.op("pool", lambda e: e.memset(Wi_bd[:], 0.0), reads=R(), writes=W("Wi_bd"))
    for hd in range(8):
        r0 = (hd % 2) * 64
        dma(Wa_bd[r0:r0 + 64, hd // 2, r0:r0 + 64], w_a[hd], R(), W("Wa_bd"), eng="pool")
        dma(Wi_bd[r0:r0 + 64, hd // 2, r0:r0 + 64], w_i[hd], R(), W("Wi_bd"), eng="pool")
    dma(w_out_lru[:], w_out[0:512, :].rearrange("(c p) n -> p c n", p=128), R(), W("w_out_lru"), eng="pool")
    dma(wo_mla[:], w_out[512:1024, :].rearrange("(h v) n -> v h n", v=64), R(), W("wo_mla"), eng="pool")
    dma(uq_f[:], w_uq.rearrange("(c p) n -> p c n", p=128), R(), W("uq_f"))
    dma(ukv_f[:], w_ukv, R(), W("ukv_f"))
    w_in_v = w_in.rearrange("(c p) n -> c p n", p=128)
    for c in range(8):
        sl = c % 2
        dma(stg_in[sl][:], w_in_v[c], R(), W(("stg_in", sl)))
        ts("dve", w_in_bf[:, c, :], stg_in[sl][:], gpm_sb[:, c:c + 1], None, ALU.mult, None,
           reads=R(("stg_in", sl), "gpm"), writes=W("w_in_bf"))
    act(lam_e[:], vecs[:, :, 7], AF.Exp, reads=R("vecs"), writes=W("lam_e"), scale=-1.0)
    act(lam_e[:], lam_e[:], AF.Ln, reads=R("lam_e", "onecol"), writes=W("lam_e"), bias=onecol[:, 0:1])
    ts("dve", sp8[:], lam_e[:], -8.0, None, ALU.mult, None, reads=R("lam_e"), writes=W("sp8"))
    uq4 = uq_f[:].rearrange("p c (h e) -> p c h e", e=96)
    ukv3 = ukv_f[:].rearrange("p (h e) -> p h e", e=128)
    for c in range(2):
        cp("dve", W_qr[:, c, :, :], uq4[:, c, :, 64:96], reads=R("uq_f"), writes=W("W_qr"))
    tcount = 0
    for c in range(2):
        for h in range(8):
            bk = 4 + (tcount % 4)
            tcount += 1
            tr(banks[bk][0:64, 0:128], uq4[:, c, h, 0:64], ident_f[:], reads=R("uq_f", "ident_f"), writes=W(BK(bk)))
            cp("dve" if h % 2 == 0 else "act", uqT[:, c, h, :], banks[bk][0:64, 0:128], reads=R(BK(bk)), writes=W("uqT"))
    for h in range(8):
        bk = 4 + (tcount % 4)
        tcount += 1
        tr(banks[bk][0:64, 0:128], ukv3[:, h, 0:64], ident_f[:], reads=R("ukv_f", "ident_f"), writes=W(BK(bk)))
        cp("dve", ukT[:, h, :], banks[bk][0:64, 0:128], reads=R(BK(bk)), writes=W("ukT"))
        bk = 4 + (tcount % 4)
        tcount += 1
        tr(banks[bk][0:64, 0:128], ukv3[:, h, 64:128], ident_f[:], reads=R("ukv_f", "ident_f"), writes=W(BK(bk)))
        cp("act", uvT[:, h, :], banks[bk][0:64, 0:128], reads=R(BK(bk)), writes=W("uvT"))
    for c in range(2):
        for h in range(8):
            bk = 4 + (tcount % 4)
            tcount += 1
            mm(banks[bk][:, 0:128], uqT[:, c, h, :], ukT[:, h, :], True, True, reads=R("uqT", "ukT"), writes=W(BK(bk)))
            cp("dve" if h % 2 == 0 else "act", W_ql[:, c, h, :], banks[bk][:, 0:128], reads=R(BK(bk)), writes=W("W_ql"))
    for h in range(8):
        for dh in range(2):
            bk = 4 + (tcount % 4)
            tcount += 1
            mm(banks[bk][:, :], uvT[:, h, :], wo_mla[:, h, dh * 512:(dh + 1) * 512], True, True,
               reads=R("uvT", "wo_mla"), writes=W(BK(bk)))
            cp("dve" if dh == 0 else "act", W_comb[:, h, dh * 512:(dh + 1) * 512], banks[bk][:, :],
               reads=R(BK(bk)), writes=W("W_comb"))
    scratch_jobs = []
    for c in range(8):
        scratch_jobs.append((wup_bf[c * 128:(c + 1) * 128, :], w_up[c * 128:(c + 1) * 128, :], "wup_bf"))
    for c in range(8):
        scratch_jobs.append((wdn_bf[c * 512:(c + 1) * 512, :], w_down[c * 512:(c + 1) * 512, :], "wdn_bf"))

    def scratch_pump(n=1):
        for _ in range(n):
            if scratch_jobs:
                o_, i_, k_ = scratch_jobs.pop(0)
                dma(o_, i_, [], [k_], eng="pool", grp="scr")

    phase_switch()

    xcnt = [0]
    tcnt = [0]

    xpre = [False]

    def norm_transpose(src_rows, col0, next_rows=None):
        sl = xcnt[0] % 2
        xcnt[0] += 1
        xt = x_in[sl]
        kx = ("x_in", sl)
        if not xpre[0]:
            dma(xt[:], src_rows, R(), W(kx))
        xpre[0] = False
        if next_rows is not None:
            nsl = xcnt[0] % 2
            dma(x_in[nsl][:], next_rows, R(), W(("x_in", nsl)))
            xpre[0] = True
        c = newstat(2)
        act(xs[:], xt[:], AF.Square, reads=R(kx), writes=W("xs", ("stat", c)), accum_out=stat[:, c:c + 1])
        rstd_from(c, D, c + 1)
        act(xs[:], xt[:], AF.Copy, reads=R(kx, ("stat", c + 1)), writes=W("xs"), scale=stat[:, c + 1:c + 2])
        for ch in range(8):
            tr(banks_bf[0][:, ch * 128:(ch + 1) * 128], xs[:, ch * 128:(ch + 1) * 128], ident_bf[:],
               reads=R("xs", "ident_bf"), writes=W(BK(0)))
        cp("dve", xnT[:, :, col0:col0 + 128], banks_bf[0][:, :].rearrange("p (c t) -> p c t", c=8),
           reads=R(BK(0)), writes=W("xnT"))

    def kv_block(col0, tab_idx, out_ckv_rows, out_kr_rows, Vd, KTd, krTd, kV, kKT, kkrT):
        kb1 = BK(4)
        for ch in range(8):
            mm(banks[4][:, 0:160], xnT[:, ch, col0:col0 + 128], w_in_bf[:, ch, 1280:1440], ch == 0, ch == 7,
               reads=R("xnT", "w_in_bf"), writes=W(kb1))
        sl = tcnt[0] % 2
        tcnt[0] += 1
        dma(tabr[sl][:], tab_real[tab_idx], R(), W(("tabr", sl)))
        c = newstat(2)
        act(kv_bf[:, 0:128], banks[4][:, 0:128], AF.Square, reads=R(kb1), writes=W("kv_bf", ("stat", c)),
            accum_out=stat[:, c:c + 1])
        rstd_from(c, 128, c + 1)
        stt(kv_f[:, 0:128], banks[4][:, 0:128], stat[:, c + 1:c + 2], gkv[:], ALU.mult, ALU.mult,
            reads=R(kb1, ("stat", c + 1), "gkv"), writes=W("kv_f"))
        cosv = tabr[sl][:, 0:16]
        sinv = tabr[sl][:, 16:32]
        kx1 = banks[4][:, 128:144]
        kx2 = banks[4][:, 144:160]
        rk = R(kb1, ("tabr", sl))
        tt("dve", rtmp[:, 0, :], kx1, cosv, ALU.mult, reads=rk, writes=W("rtmp"))
        tt("dve", rtmp[:, 1, :], kx2, sinv, ALU.mult, reads=rk, writes=W("rtmp"))
        tt("dve", rtmp[:, 2, :], kx2, cosv, ALU.mult, reads=rk, writes=W("rtmp"))
        tt("dve", rtmp[:, 3, :], kx1, sinv, ALU.mult, reads=rk, writes=W("rtmp"))
        tt("dve", kv_f[:, 128:144], rtmp[:, 0, :], rtmp[:, 1, :], ALU.subtract, reads=R("rtmp"), writes=W("kv_f"))
        tt("dve", kv_f[:, 144:160], rtmp[:, 2, :], rtmp[:, 3, :], ALU.add, reads=R("rtmp"), writes=W("kv_f"))
        dma(out_ckv_rows, kv_f[:, 0:128], R("kv_f"), W())
        dma(out_kr_rows, kv_f[:, 128:160], R("kv_f"), W())
        cp("dve", Vd, kv_f[:, 0:128], reads=R("kv_f"), writes=W(kV))
        cp("dve", kv_bf[:, 128:160], kv_f[:, 128:160], reads=R("kv_f"), writes=W("kv_bf"))
        tr(banks_bf[5][:, 0:128], Vd, ident_bf[:], reads=R(kV, "ident_bf"), writes=W(BK(5)))
        tr(banks_bf[5][0:32, 128:256], kv_bf[:, 128:160], ident_bf[:], reads=R("kv_bf", "ident_bf"), writes=W(BK(5)))
        cp("act", KTd, banks_bf[5][:, 0:128], reads=R(BK(5)), writes=W(kKT))
        cp("act", krTd, banks_bf[5][0:32, 128:256], reads=R(BK(5)), writes=W(kkrT))

    def lru(nreal, sample):
        for ct in range(4):
            bk = 4 + ct % 2
            off = 0
            kb_ = BK(bk)
            for ch in range(8):
                mm(banks[bk][:, off:off + nreal], w_in_bf[:, ch, ct * 128:(ct + 1) * 128], xnT[:, ch, 0:nreal],
                   ch == 0, ch == 7, reads=R("xnT", "w_in_bf"), writes=W(kb_))
            if not sample:
                cp("act", xl_p[:, ct, 3:3 + nreal], banks[bk][:, off:off + nreal], reads=R(kb_), writes=W(("xl", ct)))
            else:
                for s_ in range(4):
                    cp("act", xl_s[:, ct, s_, 3:35], banks[bk][:, off + s_ * 32:off + (s_ + 1) * 32],
                       reads=R(kb_), writes=W(("xl", ct)))
        stage(231)
        for ct in range(4):
            kxl = ("xl", ct)
            kxc = ("xc", ct)
            if not sample:
                src = [xl_p[:, ct, k:k + nreal] for k in range(4)]
                dst = xc[:, ct, 0:nreal]
            else:
                src = [xl_s[:, ct, :, k:k + 32] for k in range(4)]
                dst = xc[:, ct, 0:128].rearrange("p (s t) -> p s t", s=4)
            ts("dve", dst, src[3], vecs[:, ct, 3:4], vecs[:, ct, 4:5], ALU.mult, ALU.add, reads=R(kxl, "vecs"), writes=W(kxc))
            for k in (2, 1, 0):
                stt(dst, src[k], vecs[:, ct, k:k + 1], dst, ALU.mult, ALU.add, reads=R(kxl, kxc, "vecs"), writes=W(kxc))
            cp("act", xc_bf[:, ct, 0:nreal], xc[:, ct, 0:nreal], reads=R(kxc), writes=W(("xc_bf", ct)))
        stage(232)
        n_ = nreal
        for ct in range(4):
            bk = 4 + ct % 2
            kg = BK(bk)
            mm(banks[bk][:, 0:n_], Wa_bd[:, ct, :], xc_bf[:, ct, 0:n_], True, True,
               reads=R(("xc_bf", ct), "Wa_bd"), writes=W(kg))
            act(t_r[:, ct, 0:n_], banks[bk][:, 0:n_], AF.Sigmoid, reads=R(kg, "vecs"), writes=W(("t_r", ct)),
                bias=vecs[:, ct, 5:6])
            bk2 = 5 - ct % 2
            kg2 = BK(bk2)
            mm(banks[bk2][:, 0:n_], Wi_bd[:, ct, :], xc_bf[:, ct, 0:n_], True, True,
               reads=R(("xc_bf", ct), "Wi_bd"), writes=W(kg2))
            act(t_i[:, ct, 0:n_], banks[bk2][:, 0:n_], AF.Sigmoid, reads=R(kg2, "vecs"), writes=W(("t_i", ct)),
                bias=vecs[:, ct, 6:7])
        for ct in range(4):
            tt("dve", t_i[:, ct, 0:n_], t_i[:, ct, 0:n_], xc[:, ct, 0:n_], ALU.mult, reads=R(("t_i", ct), ("xc", ct)),
               writes=W(("t_i", ct)))
        TA = [t_r[:, ct, 0:n_] for ct in range(4)]
        TB = [t_i[:, ct, 0:n_] for ct in range(4)]
        TS = [xc[:, ct, 0:n_] for ct in range(4)]
        for ct in range(4):
            act(TA[ct], TA[ct], AF.Exp, reads=R(("t_r", ct), "sp8"), writes=W(("t_r", ct)), scale=sp8[:, ct:ct + 1])
        for ct in range(4):
            tt("dve", TS[ct], TA[ct], TA[ct], ALU.mult, reads=R(("t_r", ct), ("t_i", ct)), writes=W(("xc", ct)))
        for ct in range(4):
            act(TS[ct], TS[ct], AF.Ln, reads=R(("xc", ct), "onecol"), writes=W(("xc", ct)), scale=-1.0, bias=onecol[:, 0:1])
        for ct in range(4):
            act(TS[ct], TS[ct], AF.Exp, reads=R(("xc", ct)), writes=W(("xc", ct)), scale=0.5)
        for ct in range(4):
            tt("dve", TB[ct], TB[ct], TS[ct], ALU.mult, reads=R(("t_i", ct), ("xc", ct)), writes=W(("t_i", ct)))
        for ct in range(4):
            stage(236)
            if not sample:
                S.op("dve", lambda e, ct=ct: e.tensor_tensor_scan(hT[:, ct, 0:nreal], t_r[:, ct, 0:nreal], t_i[:, ct, 0:nreal],
                                                                   hstate[:, ct:ct + 1], ALU.mult, ALU.add),
                     reads=R(("t_r", ct), ("t_i", ct), ("hstate", ct)), writes=W(("hT", ct)), cost=0.65)
                cp("dve", hstate[:, ct:ct + 1], hT[:, ct, nreal - 1:nreal], reads=R(("hT", ct)), writes=W(("hstate", ct)))
                cp("pool", xl_p[:, ct, 0:3], xl_p[:, ct, nreal:nreal + 3], reads=R(("xl", ct)), writes=W(("xl", ct)))
            else:
                a3 = t_r[:, ct, 0:128].rearrange("p (s t) -> p s t", s=4)[:, :, 0]
                b3 = t_i[:, ct, 0:128].rearrange("p (s t) -> p s t", s=4)[:, :, 0]
                tt("dve", rtmp[:, 0, 0:4], a3, st_h_sb[:, ct, :], ALU.mult, reads=R(("t_r", ct), "st_h"), writes=W("rtmp"))
                tt("dve", b3, b3, rtmp[:, 0, 0:4], ALU.add, reads=R(("t_i", ct), "rtmp"), writes=W(("t_i", ct)))
                S.op("dve", lambda e, a3=a3: e.memset(a3, 0.0), reads=R("rtmp"), writes=W(("t_r", ct)))
                S.op("dve", lambda e, ct=ct: e.tensor_tensor_scan(hT[:, ct, 0:128], t_r[:, ct, 0:128], t_i[:, ct, 0:128],
                                                                   0.0, ALU.mult, ALU.add),
                     reads=R(("t_r", ct), ("t_i", ct)), writes=W(("hT", ct)), cost=0.4)

    def owned_front_q(tab_ap, BS, cqb, cql, qlb):
        kq = BK(cqb)
        for ch in range(8):
            mm(banks[cqb][:, cql:cql + 256], xnT_own[:, ch, :], w_in_bf[:, ch, 1024:1280], ch == 0, ch == 7,
               reads=R("xnT_own", "w_in_bf"), writes=W(kq))
        c = newstat(2)
        act(cqn[:], banks[cqb][:, cql:cql + 256], AF.Square, reads=R(kq), writes=W("cqn", ("stat", c)),
            accum_out=stat[:, c:c + 1])
        rstd_from(c, 256, c + 1)
        stt(cqn[:], banks[cqb][:, cql:cql + 256], stat[:, c + 1:c + 2], gq[:], ALU.mult, ALU.mult,
            reads=R(kq, ("stat", c + 1), "gq"), writes=W("cqn"))
        for cc in range(2):
            tr(banks_bf[0][:, cc * 128:(cc + 1) * 128], cqn[:, cc * 128:(cc + 1) * 128], ident_bf[:],
               reads=R("cqn", "ident_bf"), writes=W(BK(0)))
        cp("dve", cqnT[:], banks_bf[0][:, 0:256].rearrange("p (c t) -> p c t", c=2), reads=R(BK(0)), writes=W("cqnT"))
        for h in range(8):
            bk = qlb[h // 4]
            off = (h % 4) * 128
            for cc in range(2):
                mm(banks[bk][:, off:off + 128], W_ql[:, cc, h, :], cqnT[:, cc, :], cc == 0, cc == 1,
                   reads=R("W_ql", "cqnT"), writes=W(BK(bk, off, off + 128)))
        for bi, bk in enumerate(qlb):
            cp("act", BS["ql"][:, bi * 4:bi * 4 + 4, :], banks[bk][:, :].rearrange("p (h t) -> p h t", h=4),
               reads=R(BK(bk)), writes=W(BS["kql"]))
        for cc in range(2):
            mm(banks[cqb][:, cql:cql + 256], cqnT[:, cc, :], W_qr[:, cc, :, :].rearrange("p h e -> p (h e)"), cc == 0, cc == 1,
               reads=R("cqnT", "W_qr"), writes=W(kq))
        q3 = banks[cqb][:, cql:cql + 256].rearrange("p (h e) -> p h e", e=32)
        cosb = tab_ap[:, 0:16].unsqueeze(1).to_broadcast([128, 8, 16])
        sinb = tab_ap[:, 16:32].unsqueeze(1).to_broadcast([128, 8, 16])
        tt("dve", qr_r[:, :, 0:16], q3[:, :, 0:16], cosb, ALU.mult, reads=R(kq, "tabo"), writes=W("qr_r"))
        tt("dve", qr_r[:, :, 16:32], q3[:, :, 16:32], cosb, ALU.mult, reads=R(kq, "tabo"), writes=W("qr_r"))
        tt("dve", qr_s[:, :, 0:16], q3[:, :, 0:16], sinb, ALU.mult, reads=R(kq, "tabo"), writes=W("qr_s"))
        tt("dve", qr_s[:, :, 16:32], q3[:, :, 16:32], sinb, ALU.mult, reads=R(kq, "tabo"), writes=W("qr_s"))
        tt("dve", qr_bf[:, :, 0:16], qr_r[:, :, 0:16], qr_s[:, :, 16:32], ALU.subtract,
           reads=R("qr_r", "qr_s"), writes=W("qr_bf"))
        tt("dve", qr_bf[:, :, 16:32], qr_r[:, :, 16:32], qr_s[:, :, 0:16], ALU.add,
           reads=R("qr_r", "qr_s"), writes=W("qr_bf"))
        for h in range(8):
            tr(banks_bf[0][0:32, h * 128:(h + 1) * 128], qr_bf[:, h, :], ident_bf[:], reads=R("qr_bf", "ident_bf"),
               writes=W(BK(0)))
        cp("dve", BS["qr"][:], banks_bf[0][0:32, :].rearrange("p (h t) -> p h t", h=8), reads=R(BK(0)), writes=W(BS["kqr"]))

    def owned_front_bg(tab_ap, BS, gb, cqb, cql, qlb):
        for ct in range(4):
            bk = gb[ct]
            kg = BK(bk)
            for ch in range(8):
                mm(banks[bk][:, 0:128], w_in_bf[:, ch, 512 + ct * 128:512 + (ct + 1) * 128], xnT_own[:, ch, :],
                   ch == 0, ch == 7, reads=R("xnT_own", "w_in_bf"), writes=W(kg))
            g = banks[bk][:, 0:128]
            act(g_a[:, ct, :], g, AF.Square, reads=R(kg), writes=W(("g_a", ct)))
            cp("act", g_b[:, ct, :], g, reads=R(kg), writes=W(("g_b", ct)))
            ts("dve", g_a[:, ct, :], g_a[:, ct, :], 0.044715, 1.0, ALU.mult, ALU.add, reads=R(("g_a", ct)), writes=W(("g_a", ct)))
            tt("dve", g_a[:, ct, :], g_a[:, ct, :], g_b[:, ct, :], ALU.mult, reads=R(("g_a", ct), ("g_b", ct)), writes=W(("g_a", ct)))
        for ct in range(4):
            act(g_a[:, ct, :], g_a[:, ct, :], AF.Sigmoid, reads=R(("g_a", ct)), writes=W(("g_a", ct)), scale=GELU_C)
        for ct in range(4):
            tt("dve", g_b[:, ct, :], g_b[:, ct, :], g_a[:, ct, :], ALU.mult, reads=R(("g_b", ct), ("g_a", ct)), writes=W(("g_b", ct)))
            tt("dve", BS["y"][:, ct, :], g_b[:, ct, :], hsel[:, ct, :], ALU.mult, reads=R(("g_b", ct), ("hsel", ct)),
               writes=W(BS["ky"]))
        owned_front_q(tab_ap, BS, cqb, cql, qlb)

    def owned_front(tab_ap, BS, bgm):
        gb = [4, 5, 4, 5] if bgm else [4, 5, 6, 7]
        cqb, cql = (5, 0) if bgm else (1, 160)
        qlb = [4, 5] if bgm else [2, 3]
        if bgm:
            return owned_front_bg(tab_ap, BS, gb, cqb, cql, qlb)
        for ct in range(4):
            bk = gb[ct]
            kg = BK(bk)
            for ch in range(8):
                mm(banks[bk][:, 0:128], w_in_bf[:, ch, 512 + ct * 128:512 + (ct + 1) * 128], xnT_own[:, ch, :],
                   ch == 0, ch == 7, reads=R("xnT_own", "w_in_bf"), writes=W(kg))
        for ct in range(4):
            g = banks[gb[ct]][:, 0:128]
            kg = BK(gb[ct])
            act(g_a[:, ct, :], g, AF.Square, reads=R(kg), writes=W(("g_a", ct)))
            ts("dve", g_a[:, ct, :], g_a[:, ct, :], 0.044715, 1.0, ALU.mult, ALU.add, reads=R(("g_a", ct)), writes=W(("g_a", ct)))
            tt("dve", g_a[:, ct, :], g_a[:, ct, :], g, ALU.mult, reads=R(("g_a", ct), kg), writes=W(("g_a", ct)))
        for ct in range(4):
            act(g_b[:, ct, :], g_a[:, ct, :], AF.Sigmoid, reads=R(("g_a", ct)), writes=W(("g_b", ct)), scale=GELU_C)
        for ct in range(4):
            g = banks[gb[ct]][:, 0:128]
            kg = BK(gb[ct])
            tt("dve", g_b[:, ct, :], g_b[:, ct, :], g, ALU.mult, reads=R(("g_b", ct), kg), writes=W(("g_b", ct)))
            tt("dve", BS["y"][:, ct, :], g_b[:, ct, :], hsel[:, ct, :], ALU.mult, reads=R(("g_b", ct), ("hsel", ct)),
               writes=W(BS["ky"]))
        owned_front_q(tab_ap, BS, cqb, cql, qlb)

    sbank = [0]
    ptc = [0]

    SBANKS = [6, 7, 1]
    HOP = 0.6

    def sched_merge(F, B, constrained=False):
        if not B:
            return list(F)
        if not F:
            return list(B)
        eng_free = {}
        wfin, rfin = {}, {}
        from collections import Counter
        remR, remW = Counter(), Counter()

        def ksets(item):
            ops_ = item if isinstance(item, list) else [item]
            r_, w_ = set(), set()
            for o_ in ops_:
                r_.update(o_[2])
                w_.update(o_[3])
            r_.discard("PHASE")
            return r_, w_

        kF = [ksets(a) for a in F] if constrained else None
        kB = [ksets(b) for b in B] if constrained else None
        if constrained:
            for r_, w_ in kF:
                remR.update(r_)
                remW.update(w_)

        def start_of(item, commit):
            ops_ = item if isinstance(item, list) else [item]
            local_free = eng_free if commit else dict(eng_free)
            lw = wfin if commit else {}
            lr = rfin if commit else {}

            def getw(k):
                v = lw.get(k)
                return v if v is not None else wfin.get(k)

            def getr(k):
                v = lr.get(k)
                return v if v is not None else rfin.get(k)

            first = None
            for (eng, _f, reads_, writes_, dma_, _g, cost_) in ops_:
                rdy = 0.0
                for k in reads_:
                    if k == "PHASE":
                        continue
                    v = getw(k)
                    if v is not None:
                        rdy = max(rdy, v[0] + (HOP if v[1] != eng else 0.0))
                for k in writes_:
                    v = getw(k)
                    if v is not None:
                        rdy = max(rdy, v[0] + (HOP if v[1] != eng else 0.0))
                    v = getr(k)
                    if v is not None:
                        rdy = max(rdy, v[0] + (HOP if v[1] != eng else 0.0))
                st = max(local_free.get(eng, 0.0), rdy)
                if first is None:
                    first = st
                fin = st + cost_
                local_free[eng] = st + (0.15 if dma_ else cost_)
                for k in reads_:
                    if k != "PHASE":
                        o_ = getr(k)
                        if o_ is None or o_[0] < fin:
                            lr[k] = (fin, eng if not dma_ else "dma")
                for k in writes_:
                    lw[k] = (fin, eng if not dma_ else "dma")
            return first

        out = []
        i = j = 0
        while i < len(F) or j < len(B):
            if i >= len(F):
                pick = "b"
            elif j >= len(B):
                pick = "f"
            else:
                ok = True
                if constrained:
                    r_, w_ = kB[j]
                    ok = not (any(remR[k] > 0 or remW[k] > 0 for k in w_) or any(remW[k] > 0 for k in r_))
                if not ok:
                    pick = "f"
                else:
                    sf = start_of(F[i], False)
                    sb_ = start_of(B[j], False)
                    pick = "f" if sf <= sb_ else "b"
            if pick == "f":
                start_of(F[i], True)
                out.append(F[i])
                if constrained:
                    remR.subtract(kF[i][0])
                    remW.subtract(kF[i][1])
                i += 1
            else:
                start_of(B[j], True)
                out.append(B[j])
                j += 1
        return out

    def attn(groups, kblocks, bg=None, constrained=False):
        nk = len(kblocks)
        units = [(ki, kb, gi, g) for ki, kb in enumerate(kblocks) for gi, g in enumerate(groups)]
        LA = 2
        pend = []

        def issue_S(ki, kb, gi, g):
            N = g["N"]
            bk = SBANKS[sbank[0] % 3]
            sbank[0] += 1
            st = banks[bk][:, 0:N]
            kst = BK(bk)
            mk = kb["masks"][gi] if kb["masks"] is not None else None
            mm(st, kb["KT"], g["ql"], True, False, reads=R(g["kql"], *kb["keys"]), writes=W(kst))
            mm(st, kb["krT"], g["qr"], False, mk is None, reads=R(g["kqr"], *kb["keys"]), writes=W(kst))
            if mk is not None:
                mm(st, ident_bf[:], mk, False, True, reads=R("ident_bf", "maskb", "masks"), writes=W(kst))
            pi = ptc[0] % 3
            ptc[0] += 1
            pt = PT[pi][:, 0:N]
            act(pt, st, AF.Exp, reads=R(kst), writes=W(("PT", pi)), scale=ATTN_SCALE)
            return (ki, kb, g, pt, pi)

        def issue_PV(ki, kb, g, pt, pi):
            N = g["N"]
            mm(g["O"], kb["V"], pt, ki == 0, ki == nk - 1, reads=R(("PT", pi), *kb["keys"]), writes=W(g["kO"]), noatom=True)
            la = Lacc[g["li"]][:, 0:N]
            kla = ("Lacc", g["li"])
            le = "pool" if (g["li"] == 1 and LACC_POOL) else "dve"
            if ki == 0:
                cp(le, la, pt, reads=R(("PT", pi)), writes=W(kla))
            else:
                tt(le, la, la, pt, ALU.add, reads=R(("PT", pi), kla), writes=W(kla))

        prev_def = S.defer
        S.defer = []
        for u in units:
            pend.append(issue_S(*u))
            if len(pend) > LA:
                issue_PV(*pend.pop(0))
        while pend:
            issue_PV(*pend.pop(0))
        fg_list = S.defer
        S.defer = prev_def
        merged = sched_merge(fg_list, bg if bg else [], constrained=constrained)
        if bg:
            bg[:] = []
        S.flush(merged, len(merged))
        for g in groups:
            N = g["N"]
            kla = ("Lacc", g["li"])
            rv = Lacc[g["li"]][:, 0:N]
            mm(g["L"], ones_f[:], rv, True, True, reads=R(kla, "ones_f"), writes=W(g["kL"]))
            act(rv, g["L"], AF.Ln, reads=R(g["kL"]), writes=W(kla))
            act(rv, rv, AF.Exp, reads=R(kla), writes=W(kla), scale=-1.0)
            g["fin"](rv, kla)

    def mix_and_prep(x_rows, slot, BS, bgm=False):
        if x_rows is not None:
            dma(x1[:, slot, :], x_rows, R(), W(("x1", slot)))
        c = newstat(4)
        mb = [4, 5] if bgm else [6, 7]
        for dh in range(2):
            bk = mb[dh]
            kk = BK(bk)
            for ct in range(4):
                mm(banks[bk][:, :], BS["y"][:, ct, :], w_out_lru[:, ct, dh * 512:(dh + 1) * 512], ct == 0, False,
                   reads=R(BS["ky"], "w_out_lru"), writes=W(kk))
            for h in range(8):
                mm(banks[bk][:, :], olatT[:, h, :], W_comb[:, h, dh * 512:(dh + 1) * 512], False, h == 7,
                   reads=R("olatT", "W_comb"), writes=W(kk))
            act(mix_sb[:, dh * 512:(dh + 1) * 512], banks[bk][:, :], AF.Square, reads=R(kk),
                writes=W("mix_sb", ("stat", c + dh)), accum_out=stat[:, c + dh:c + dh + 1])
        tt("dve", stat[:, c + 2:c + 3], stat[:, c:c + 1], stat[:, c + 1:c + 2], ALU.add,
           reads=R(("stat", c), ("stat", c + 1)), writes=W(("stat", c + 2)))
        rstd_from(c + 2, D, c + 3)
        for dh in range(2):
            bk = mb[dh]
            stt(mix_sb[:, dh * 512:(dh + 1) * 512], banks[bk][:, :], stat[:, c + 3:c + 4],
                gpostmix[:, dh * 512:(dh + 1) * 512], ALU.mult, ALU.mult,
                reads=R(BK(bk), ("stat", c + 3), "gpostmix"), writes=W("mix_sb"))
        tt("dve", x1[:, slot, :], x1[:, slot, :], mix_sb[:], ALU.add, reads=R(("x1", slot), "mix_sb"), writes=W(("x1", slot)))
        c2 = newstat(2)
        msb = mix_sb[:].bitcast(BF16)
        xs2v = msb[:, 0:1024]
        act(msb[:, 1024:2048], x1[:, slot, :], AF.Square, reads=R(("x1", slot), "mix_sb"), writes=W("mix_sb", ("stat", c2)),
            accum_out=stat[:, c2:c2 + 1])
        rstd_from(c2, D, c2 + 1)
        stt(xs2v, x1[:, slot, :], stat[:, c2 + 1:c2 + 2], gpremlp[:], ALU.mult, ALU.mult,
            reads=R(("x1", slot), ("stat", c2 + 1), "gpremlp"), writes=W("mix_sb"))
        tb_ = 4 if bgm else 0
        for ch in range(8):
            tr(banks_bf[tb_][:, ch * 128:(ch + 1) * 128], xs2v[:, ch * 128:(ch + 1) * 128], ident_bf[:],
               reads=R("mix_sb", "ident_bf"), writes=W(BK(tb_)))
        cp("act", xn2T[:, :, slot * 128:(slot + 1) * 128], banks_bf[tb_][:, :].rearrange("p (c t) -> p c t", c=8),
           reads=R(BK(tb_)), writes=W(("xn2T", slot)))

    wupc = [0]
    wdnc = [0]
    w_up_v = wup_bf.rearrange("(c p) f -> p c f", p=128)
    w_dn_v = wdn_bf.rearrange("(fc p) d -> p fc d", p=128)

    def ffn(nb, out_rows_fn):
        ntok = nb * 128
        for g in range(16):
            sl = wupc[0] % NWS
            wupc[0] += 1
            dma(wup[sl][:], w_up_v[:, :, g * 256:(g + 1) * 256], ["wup_bf"], [("wup", sl)])
            for f2 in range(2):
                fc = 2 * g + f2
                bk = fc % 2
                for ch in range(8):
                    mm(banks[bk][:, 0:ntok], wup[sl][:, ch, f2 * 128:(f2 + 1) * 128], xn2T[:, ch, 0:ntok], ch == 0, ch == 7,
                       reads=R(("wup", sl), *[("xn2T", j) for j in range(nb)]), writes=W(BK(bk)))
                act(rl[bk][:, 0:ntok], banks[bk][:, 0:ntok], AF.Relu, reads=R(BK(bk)), writes=W(("rl", bk)))
                tt("pool", hid[:, fc, 0:ntok], rl[bk][:, 0:ntok], rl[bk][:, 0:ntok], ALU.mult, reads=R(("rl", bk)),
                   writes=W(("hid", fc)))
        c = newstat(16)
        for dh in range(2):
            for fg in range(8):
                sl = wdnc[0] % NWS
                wdnc[0] += 1
                dma(wdn[sl][:], w_dn_v[:, fg * 4:(fg + 1) * 4, dh * 512:(dh + 1) * 512], ["wdn_bf"], [("wdn", sl)])
                for fl in range(4):
                    fc = fg * 4 + fl
                    for t_ in range(nb):
                        mm(banks[2 + t_][:, :], hid[:, fc, t_ * 128:(t_ + 1) * 128], wdn[sl][:, fl, :], fc == 0, fc == 31,
                           reads=R(("hid", fc), ("wdn", sl)), writes=W(BK(2 + t_)))
            for t_ in range(nb):
                cc = c + t_ * 4 + dh
                act(ff_sb[:, t_, dh * 512:(dh + 1) * 512], banks[2 + t_][:, :], AF.Square, reads=R(BK(2 + t_)),
                    writes=W(("ff_sb", t_), ("stat", cc)), accum_out=stat[:, cc:cc + 1])
                cp("dve", ff_sb[:, t_, dh * 512:(dh + 1) * 512], banks[2 + t_][:, :], reads=R(BK(2 + t_)),
                   writes=W(("ff_sb", t_)))
        for t_ in range(nb):
            cc = c + t_ * 4
            tt("dve", stat[:, cc + 2:cc + 3], stat[:, cc:cc + 1], stat[:, cc + 1:cc + 2], ALU.add,
               reads=R(("stat", cc), ("stat", cc + 1)), writes=W(("stat", cc + 2)))
            rstd_from(cc + 2, D, cc + 3)
            yt = ff_sb[:, t_, :]
            ky = ("ff_sb", t_)
            stt(yt, yt, stat[:, cc + 3:cc + 4], gpostmlp[:], ALU.mult, ALU.mult,
                reads=R(ky, ("stat", cc + 3), "gpostmlp"), writes=W(ky))
            tt("dve", yt, yt, x1[:, t_, :], ALU.add, reads=R(ky, ("x1", t_)), writes=W(ky))
            dma(out_rows_fn(t_), yt, R(ky), W())

    def record(fn_, *a_):
        prev = S.defer
        S.defer = []
        fn_(*a_)
        lst = S.defer
        S.defer = prev
        return lst

    def sample_prep(bgm):
        XLK_ = [("xl", ct) for ct in range(4)]
        HTK_ = [("hT", ct) for ct in range(4)]
        dma(st_h_sb[:], st_h, R(), W("st_h"))
        dma(xl_s[:, :, :, 0:3], st_conv, R(), W(XLK_))
        xcnt[0] = 0
        xpre[0] = False
        norm_transpose(xs_tok, 0)
        kv_block(0, NB, ckv_s, kr_s, Vn[:], KTn[:], krTn[:], "Vn", "KTn", "krTn")
        lru(128, True)
        cp("dve", hs_sb[:], hT[:, :, 0:128].rearrange("p c (s t) -> p c s t", s=4)[:, :, :, 31], reads=R(HTK_), writes=W("hs_sb"))
        dma(h_s, hs_sb[:], R("hs_sb"), W())
        dma(conv_s, xl_s[:, :, :, 32:35], R(XLK_), W())
        cp("dve", xnT_own[:], xnT[:, :, 0:128], reads=R("xnT"), writes=W("xnT_own"))
        for ct in range(4):
            cp("dve", hsel[:, ct, :], hT[:, ct, 0:128], reads=R(("hT", ct)), writes=W(("hsel", ct)))
        dma(tabo[:], tab_real[NB], R(), W("tabo"))
        owned_front(tabo, BSS[0], bgm)

    BSS = [dict(y=y_lruT, ky="y_lruT", ql=qlatT, kql="qlatT", qr=qrT, kqr="qrT"),
           dict(y=y_lruT2, ky=("wdn", 0), ql=qlatT2, kql=("wdn", 0), qr=qrT2, kqr=("wdn", 1))]

    def program():
        stage(1)
        XLK = [("xl", ct) for ct in range(4)]
        HTK = [("hT", ct) for ct in range(4)]
        VK = [("V", kb) for kb in range(NB)]
        S.op("dve", lambda e: e.memset(xl_p[:, :, 0:3], 0.0), reads=R(), writes=W(XLK))
        xp_all_v = xp_all.rearrange("(b p) d -> b p d", p=128)
        xp_own_v = xp_own.rearrange("(b p) d -> b p d", p=128)
        y_own_v = y_own.rearrange("(b p) d -> b p d", p=128)
        ckv_p_v = ckv_p.rearrange("(b p) d -> b p d", p=128)
        kr_p_v = kr_p.rearrange("(b p) d -> b p d", p=128)
        for stile in range(4):
            bg = []

            def keysets(item):
                ops_ = item if isinstance(item, list) else [item]
                r_, w_ = set(), set()
                for (_e, _f, reads_, writes_, _d, _g, _c) in ops_:
                    r_.update(reads_)
                    w_.update(writes_)
                r_.discard("PHASE")
                return r_, w_

            def safe_merge(A, B):
                if not SAFE_MERGE:
                    return A + B
                if TIMED_MERGE:
                    return sched_merge(A, B, constrained=True)
                from collections import Counter
                remR, remW = Counter(), Counter()
                ka = [keysets(a) for a in A]
                kb_ = [keysets(b) for b in B]
                for r_, w_ in ka:
                    remR.update(r_)
                    remW.update(w_)
                out = []
                ia = ib = 0
                while ia < len(A) or ib < len(B):
                    if ia < len(A):
                        out.append(A[ia])
                        remR.subtract(ka[ia][0])
                        remW.subtract(ka[ia][1])
                        ia += 1
                    if ib < len(B):
                        r_, w_ = kb_[ib]
                        conflict = any(remR[k] > 0 or remW[k] > 0 for k in w_) or any(remW[k] > 0 for k in r_)
                        if not conflict or ia >= len(A):
                            out.append(B[ib])
                            ib += 1
                return out

            def record(fn_, *a_):
                prev = S.defer
                S.defer = []
                fn_(*a_)
                lst = S.defer
                S.defer = prev
                return lst

            def phaseA_parts(p_, j_, load_x1=True, last_prefetch=True):
                parts = []
                if load_x1:
                    parts.append(record(lambda: dma(x1[:, j_, :], xp_own_v[p_], R(), W(("x1", j_)))))
                for rb in range(2):
                    b = 2 * p_ + rb
                    nxt = xp_all_v[b + 1] if (b + 1 < NB and (rb == 0 or last_prefetch)) else None

                    def one(b=b, rb=rb, nxt=nxt):
                        norm_transpose(xp_all_v[b], rb * 128, nxt)
                        kv_block(rb * 128, b, ckv_p_v[b], kr_p_v[b], V[:, b, :], KT[:, b * 128:(b + 1) * 128],
                                 krT[:, b * 128:(b + 1) * 128], ("V", b), ("KT", b), ("krT", b))
                    parts.append(record(one))

                def l_():
                    lru(256, False)
                    if p_ == NOWN - 1:
                        dma(h_p, hstate[:], R([("hstate", ct) for ct in range(4)]), W())
                        dma(conv_p, xl_p[:, :, 0:3], R(XLK), W())
                parts.append(record(l_))
                return parts

            def front(p_, BS, bgm):
                ts("dve", xnT_own[:], xnT[:, :, 0:128], sel[:, 0:1], None, ALU.mult, None, reads=R("xnT", "sel"), writes=W("xnT_own"))
                stt(xnT_own[:], xnT[:, :, 128:256], sel[:, 1:2], xnT_own[:], ALU.mult, ALU.add,
                    reads=R("xnT", "sel", "xnT_own"), writes=W("xnT_own"))
                for ct in range(4):
                    ts("dve", hsel[:, ct, :], hT[:, ct, 0:128], sel[:, 0:1], None, ALU.mult, None, reads=R(("hT", ct), "sel"),
                       writes=W(("hsel", ct)))
                    stt(hsel[:, ct, :], hT[:, ct, 128:256], sel[:, 1:2], hsel[:, ct, :], ALU.mult, ALU.add,
                        reads=R(("hT", ct), "sel", ("hsel", ct)), writes=W(("hsel", ct)))
                dma(tabo[:], tab_own[p_], R(), W("tabo"))
                owned_front(tabo, BS, bgm)

            for j in range(4):
                p = stile * 4 + j
                BS = BSS[j % 2]
                if j == 0:
                    if stile == 0 or not CROSS_STILE:
                        m_ = []
                        for part in phaseA_parts(p, j) + [record(front, p, BS, False)]:
                            m_ = safe_merge(m_, part)
                        S.flush(m_, len(m_))
                        scratch_pump(2)
                    else:
                        dma(x1[:, 0, :], xp_own_v[p], R(), W(("x1", 0)))
                kbl = []
                nkb = 2 * p + 2
                for kb in range(nkb):
                    mk = None
                    if kb >= nkb - 2:
                        mi = kb - (nkb - 2)
                        mk = [maskb[:, mi, :], maskb[:, mi, :]]
                    kbl.append(dict(KT=KT[:, kb * 128:(kb + 1) * 128], krT=krT[:, kb * 128:(kb + 1) * 128], V=V[:, kb, :],
                                    keys=[("KT", kb), ("krT", kb), ("V", kb)], masks=mk))
                groups = []
                for hh in range(2):
                    def fin_p(rv, krv, hh=hh):
                        tt("dve", olatT[:, hh * 4:(hh + 1) * 4, :].rearrange("p h t -> p (h t)"), banks[2 + hh][:, :], rv, ALU.mult,
                           reads=R(BK(2 + hh), krv), writes=W("olatT"))
                    groups.append(dict(
                        ql=BS["ql"][:, hh * 4:(hh + 1) * 4, :].rearrange("p h t -> p (h t)"),
                        qr=BS["qr"][:, hh * 4:(hh + 1) * 4, :].rearrange("p h t -> p (h t)"), N=512,
                        kql=BS["kql"], kqr=BS["kqr"],
                        O=banks[2 + hh][:, :], L=banks[4 + hh][:, :], kO=BK(2 + hh), kL=BK(4 + hh), fin=fin_p, li=hh))
                if j < 3 or (CROSS_STILE and stile < 3):
                    jn = (j + 1) % 4
                    m_ = list(bg)
                    for part in (phaseA_parts(p + 1, jn, load_x1=(j < 3), last_prefetch=(j < 2))
                                 + [record(front, p + 1, BSS[jn % 2], True)]):
                        m_ = safe_merge(m_, part)
                    bg[:] = m_
                elif SAMPLE_PREP_BG and stile == 3 and j == 3:
                    bg[:] = safe_merge(list(bg), record(sample_prep, True))
                attn(groups, kbl, bg)
                S.flush(bg, len(bg))
                scratch_pump(2 if stile == 0 else 0)
                if j < 3 and DEFER_TAIL:
                    bg.extend(record(mix_and_prep, None, j, BS, True))
                else:
                    mix_and_prep(None, j, BS, False)
                stage(100 + p)
            scratch_pump(16)
            phase_switch()
            ffn(4, lambda t_, stile=stile: y_own_v[stile * 4 + t_])
            phase_switch()
            stage(200 + stile)

        if not SAMPLE_PREP_BG:
            sample_prep(False)
        stage(3)
        def stream_prep(s):
            cv_ = cckv[s].rearrange("(b p) r -> p b r", p=128)
            ck_ = ckr[s].rearrange("(b p) r -> p b r", p=128)
            tb_ = 4 + (s % 2)
            for q8 in range(4):
                dma(V[:, q8 * 8:(q8 + 1) * 8, :], cv_[:, q8 * 8:(q8 + 1) * 8, :], R(), W([("V", q8 * 8 + j) for j in range(8)]),
                    eng="pool")
                dma(krc[:, q8 * 8:(q8 + 1) * 8, :], ck_[:, q8 * 8:(q8 + 1) * 8, :], R(), W(("krc", q8)), eng="pool")
            for q8 in range(4):
                for j in range(8):
                    kb = q8 * 8 + j
                    tr(banks_bf[0][:, j * 128:(j + 1) * 128], V[:, kb, :], ident_bf[:], reads=R(("V", kb), "ident_bf"),
                       writes=W(BK(0)))
                cp("dve", KT[:, q8 * 1024:(q8 + 1) * 1024], banks_bf[0][:, :], reads=R(BK(0)),
                   writes=W([("KT", q8 * 8 + j) for j in range(8)]))
                for j in range(8):
                    kb = q8 * 8 + j
                    tr(banks_bf[tb_][0:32, j * 128:(j + 1) * 128], krc[:, kb, :], ident_bf[:], reads=R(("krc", q8), "ident_bf"),
                       writes=W(BK(tb_)))
                cp("act", krT[:, q8 * 1024:(q8 + 1) * 1024], banks_bf[tb_][0:32, :], reads=R(BK(tb_)),
                   writes=W([("krT", q8 * 8 + j) for j in range(8)]))

        stream_prep(0)
        for s in range(4):
            kbl = []
            for kb in range(NB):
                kbl.append(dict(KT=KT[:, kb * 128:(kb + 1) * 128], krT=krT[:, kb * 128:(kb + 1) * 128], V=V[:, kb, :],
                                keys=[("KT", kb), ("krT", kb), ("V", kb)], masks=None))
            kbl.append(dict(KT=KTn[:], krT=krTn[:], V=Vn[:], keys=["KTn", "krTn", "Vn"], masks=[masks[:, s, :]]))
            bo = 2 + s % 2
            oo = 0

            def fin_s(rv, krv, s=s, bo=bo, oo=oo):
                tt("dve", olatT[:, :, s * 32:(s + 1) * 32], banks[bo][:, oo:oo + 256].rearrange("p (h t) -> p h t", h=8),
                   rv.rearrange("p (h t) -> p h t", h=8), ALU.mult, reads=R(BK(bo, oo, oo + 256), krv), writes=W("olatT"))

            grp = dict(ql=qlatT[:, :, s * 32:(s + 1) * 32], qr=qrT[:, :, s * 32:(s + 1) * 32], N=256,
                       O=banks[bo][:, oo:oo + 256], L=banks[bo + 2][:, oo:oo + 256],
                       kO=BK(bo, oo, oo + 256), kL=BK(bo + 2, oo, oo + 256), fin=fin_s, li=s % 2,
                       kql="qlatT", kqr="qrT")
            nxt_ = record(stream_prep, s + 1) if s < 3 else []
            attn([grp], kbl, nxt_, constrained=True)
            stage(40 + s)
        mix_and_prep(xs_tok, 0, BSS[0], False)
        stage(5)
        phase_switch()
        ffn(1, lambda t_: y_s)
        phase_switch()
        stage(6)

    try:
        program()
    except _Stop:
        pass

    S.finalize()
    return nc, S


def _emit(nc, S):
    import contextlib
    with contextlib.ExitStack() as es:
        esems = {e: es.enter_context(nc.semaphore(f"s_{e}")) for e in Sched.ENGS}
        dsems = {q: [es.enter_context(nc.semaphore(f"d_{q}{i}")) for i in range(n)] for q, n in S.n_dma.items()}
        block = es.enter_context(nc.Block())

        @block.tensor
        def _(pe):
            S.emit("pe", pe, esems, dsems)

        @block.scalar
        def _(a):
            S.emit("act", a, esems, dsems)

        @block.vector
        def _(v):
            S.emit("dve", v, esems, dsems)

        @block.gpsimd
        def _(g):
            S.emit("pool", g, esems, dsems)
            for q in ("pool", "scr"):
                st = S.dma_state[q]
                for i in range(S.n_dma[q]):
                    if st["cnt"][i]:
                        g.wait_ge(dsems[q][i], 16 * st["cnt"][i])

        @block.sync
        def _(sp):
            S.emit("sp", sp, esems, dsems)
            st = S.dma_state["sp"]
            for i in range(S.n_dma["sp"]):
                if st["cnt"][i]:
                    sp.wait_ge(dsems["sp"][i], 16 * st["cnt"][i])
    return nc


_CACHE = {}


def _rope_tab(pos):
    inv = 10000.0 ** (-np.arange(0, 32, 2, dtype=np.float64) / 32.0)
    ang = pos.astype(np.float64)[:, None] * inv[None, :]
    return np.concatenate([np.cos(ang), np.sin(ang)], axis=1).astype(np.float32)


def kernel(x_prompt, x_sample, cache_ckv, cache_krope, state_lru_h, state_conv,
           norm_pre_mix, norm_post_mix, norm_pre_mlp, norm_post_mlp, w_in, conv_w, conv_b,
           lru_w_a, lru_b_a, lru_w_i, lru_b_i, lru_lambda, q_norm, w_uq, kv_norm, w_ukv,
           w_out, w_up, w_down):
    f = lambda a: np.ascontiguousarray(np.asarray(a, dtype=np.float32))
    x_prompt, x_sample = f(x_prompt), f(x_sample)
    cache_ckv, cache_krope = f(cache_ckv)[0], f(cache_krope)[0]
    state_lru_h, state_conv = f(state_lru_h)[0], f(state_conv)[0]
    if "nc" not in _CACHE:
        nc, S = build_nc()
        _emit(nc, S)
        _CACHE["nc"] = nc
    nc = _CACHE["nc"]

    rep = lambda v, n=128: np.ascontiguousarray(np.broadcast_to(f(v).reshape(1, -1), (n, f(v).size)))
    vec = np.zeros((128, 4, 8), np.float32)
    cw = f(conv_w)[0]
    for k in range(4):
        vec[:, :, k] = cw[k].reshape(4, 128).T
    vec[:, :, 4] = f(conv_b)[0].reshape(4, 128).T
    vec[:, :, 5] = f(lru_b_a)[0].reshape(4, 128).T
    vec[:, :, 6] = f(lru_b_i)[0].reshape(4, 128).T
    vec[:, :, 7] = f(lru_lambda)[0].reshape(4, 128).T
    gpm = np.ascontiguousarray(f(norm_pre_mix)[0].reshape(8, 128).T)
    shared = dict(
        w_in=f(w_in)[0], w_out=f(w_out)[0], w_up=f(w_up)[0], w_down=f(w_down)[0],
        w_uq=f(w_uq)[0].reshape(256, 768), w_ukv=f(w_ukv)[0].reshape(128, 1024),
        w_a=f(lru_w_a)[0], w_i=f(lru_w_i)[0], vec=vec, gpm=gpm,
        g_post_mix=rep(norm_post_mix), g_pre_mlp=rep(norm_pre_mlp), g_post_mlp=rep(norm_post_mlp),
        g_q=rep(q_norm), g_kv=rep(kv_norm), ident=np.eye(128, dtype=np.float32),
    )
    masks = np.full((128, 4, 256), NEG, np.float32)
    for s in range(4):
        masks[s * 32:(s + 1) * 32, s, :] = 0.0
    shared["masks"] = masks
    pos_s = SEQ + np.tile(np.arange(32), 4)
    in_maps = []
    for c in range(NCORES):
        seq, par = c // 2, c % 2
        xa = x_prompt[seq]
        own_blocks = [2 * p + par for p in range(NOWN)]
        xo = xa.reshape(NB, 128, D)[own_blocks].reshape(NOWN * 128, D)
        diag = np.zeros((128, 128), np.float32)
        diag[64:, :64] = NEG
        full = np.full((128, 128), NEG, np.float32)
        vis = np.zeros((128, 128), np.float32)
        m0, m1 = (diag, full) if par == 0 else (vis, diag)
        maskb = np.stack([np.tile(m0, (1, 4)), np.tile(m1, (1, 4))], axis=1)
        sel = np.zeros((128, 2), np.float32)
        sel[:, par] = 1.0
        tab_real = np.zeros((NB + 1, 128, 32), np.float32)
        tab_real[:NB] = _rope_tab(np.arange(SEQ)).reshape(NB, 128, 32)
        tab_real[NB] = _rope_tab(pos_s)
        tab_own = tab_real[own_blocks]
        ss = slice(4 * c, 4 * c + 4)
        sth = state_lru_h[ss]
        stc = state_conv[ss]
        m = dict(shared)
        m.update(
            xp_all=np.ascontiguousarray(xa), xp_own=np.ascontiguousarray(xo),
            xs_tok=np.ascontiguousarray(x_sample[ss].reshape(128, D)),
            cckv=np.ascontiguousarray(cache_ckv[ss]), ckr=np.ascontiguousarray(cache_krope[ss]),
            st_h=np.ascontiguousarray(sth.reshape(4, 4, 128).transpose(2, 1, 0)),
            st_conv=np.ascontiguousarray(stc.reshape(4, 3, 4, 128).transpose(3, 2, 0, 1)),
            maskb=np.ascontiguousarray(maskb), sel=sel,
            tab_real=tab_real, tab_own=np.ascontiguousarray(tab_own),
        )
        in_maps.append(m)
    res = run_bass_kernel_spmd(nc, in_maps, core_ids=list(range(NCORES)))
    R_ = res.results
    y_p = np.zeros((4, SEQ, D), np.float32)
    y_s = np.zeros((32, 32, D), np.float32)
    ckv_p = np.zeros((1, 4, SEQ, 128), np.float32)
    kr_p = np.zeros((1, 4, SEQ, 32), np.float32)
    h_p = np.zeros((1, 4, 512), np.float32)
    cv_p = np.zeros((1, 4, 3, 512), np.float32)
    ckv_s = np.zeros((1, 32, 32, 128), np.float32)
    kr_s = np.zeros((1, 32, 32, 32), np.float32)
    h_s = np.zeros((1, 32, 512), np.float32)
    cv_s = np.zeros((1, 32, 3, 512), np.float32)
    for c in range(NCORES):
        seq, par = c // 2, c % 2
        r = R_[c]
        yo = np.asarray(r["y_own"]).reshape(NOWN, 128, D)
        yv = y_p[seq].reshape(NB, 128, D)
        for p in range(NOWN):
            yv[2 * p + par] = yo[p]
        ss = slice(4 * c, 4 * c + 4)
        y_s[ss] = np.asarray(r["y_s"]).reshape(4, 32, D)
        if par == 0:
            ckv_p[0, seq] = np.asarray(r["ckv_p"])
            kr_p[0, seq] = np.asarray(r["kr_p"])
            h_p[0, seq] = np.asarray(r["h_p"]).T.reshape(512)
            cv_p[0, seq] = np.asarray(r["conv_p"]).transpose(2, 1, 0).reshape(3, 512)
        ckv_s[0, ss] = np.asarray(r["ckv_s"]).reshape(4, 32, 128)
        kr_s[0, ss] = np.asarray(r["kr_s"]).reshape(4, 32, 32)
        h_s[0, ss] = np.asarray(r["h_s"]).transpose(2, 1, 0).reshape(4, 512)
        cv_s[0, ss] = np.asarray(r["conv_s"]).transpose(2, 3, 1, 0).reshape(4, 3, 512)
    return (y_p, y_s, ckv_p, kr_p, h_p, cv_p, ckv_s, kr_s, h_s, cv_s)
```

```python
import numpy as np
import concourse.bass as bass
import concourse.mybir as mybir
from concourse.bass_utils import run_bass_kernel_spmd

F32 = mybir.dt.float32
BF16 = mybir.dt.bfloat16
AF = mybir.ActivationFunctionType
ALU = mybir.AluOpType

NCORES = 8
D = 1024
NIN = 1440
SEQ = 4096
NB = 32
NOWN = 16
H = 8
EPS = 1e-6
ATTN_SCALE = 96.0 ** -0.5
NEG = -30000.0
GELU_C = 1.5957691216057308
STAGE = None
SAME_ENG_DIST = 8
BACKGROUND = True
DEFER_TAIL = True
LACC_POOL = False
SAFE_MERGE = True
CROSS_STILE = True
SAMPLE_PREP_BG = True
TIMED_MERGE = False


class _Stop(Exception):
    pass


class _Op:
    __slots__ = ("eng", "fn", "idx", "dma", "deps", "inc", "dsem", "dtarget", "waits")


class Sched:
    ENGS = ("pe", "act", "dve", "pool", "sp")

    def __init__(self, n_dma_sems):
        self.ops = {e: [] for e in self.ENGS}
        self.last_w = {}
        self.readers = {}
        self.n_dma = n_dma_sems
        self.dma_state = {e: {"next": 0, "last": [None] * n_dma_sems[e], "cnt": [0] * n_dma_sems[e]}
                          for e in n_dma_sems}
        self.all_dma = []
        self.defer = None

    def op(self, eng, fn, reads=(), writes=(), dma=False, grp=None, atom=None, cost=0.3):
        if self.defer is not None:
            item = (eng, fn, list(reads), list(writes), dma, grp, cost)
            if atom == "begin":
                self.defer.append([item])
            elif atom in ("mid", "end"):
                self.defer[-1].append(item)
            else:
                self.defer.append(item)
            return None
        return self._op(eng, fn, reads, writes, dma, grp)

    def flush(self, lst, n):
        k = 0
        while lst and k < n:
            it = lst.pop(0)
            if isinstance(it, list):
                for sub in it:
                    self._op(*sub[:6])
            else:
                self._op(*it[:6])
            k += 1

    def _op(self, eng, fn, reads=(), writes=(), dma=False, grp=None):
        o = _Op()
        o.eng, o.fn, o.dma, o.inc = eng, fn, dma, False
        o.idx = len(self.ops[eng])
        deps = {}
        for k in reads:
            w = self.last_w.get(k)
            if w is not None:
                deps[(w.eng, w.idx)] = (w, True)
        for k in writes:
            w = self.last_w.get(k)
            if w is not None and (w.eng, w.idx) not in deps:
                deps[(w.eng, w.idx)] = (w, False)
            for r in self.readers.get(k, ()):
                if (r.eng, r.idx) not in deps:
                    deps[(r.eng, r.idx)] = (r, False)
        if dma:
            q = grp if grp is not None else eng
            st = self.dma_state[q]
            slot = st["next"] % self.n_dma[q]
            st["next"] += 1
            prev = st["last"][slot]
            if prev is not None:
                deps[(prev.eng, prev.idx)] = (prev, True)
            st["last"][slot] = o
            st["cnt"][slot] += 1
            o.dsem = (q, slot)
            o.dtarget = 16 * st["cnt"][slot]
            self.all_dma.append(o)
        o.deps = list(deps.values())
        for k in reads:
            self.readers.setdefault(k, []).append(o)
        for k in writes:
            self.last_w[k] = o
            self.readers[k] = []
        self.ops[eng].append(o)
        return o

    def finalize(self):
        for e in self.ENGS:
            for o in self.ops[e]:
                o.waits = []
                for (d, raw) in o.deps:
                    if d.dma:
                        o.waits.append(("dma", d))
                    elif d.eng != o.eng:
                        d.inc = True
                        o.waits.append(("eng", d))
                    else:
                        if (o.idx - d.idx <= SAME_ENG_DIST and o.eng != "pe") or o.dma or o.eng == "pool":
                            d.inc = True
                            o.waits.append(("eng", d))
        self.cum = {}
        for e in self.ENGS:
            c = 0
            arr = []
            for o in self.ops[e]:
                if o.inc and not o.dma:
                    c += 1
                arr.append(c)
            self.cum[e] = arr

    def emit(self, eng_name, engine, esems, dsems):
        maxw = {}
        for o in self.ops[eng_name]:
            for kind, d in o.waits:
                if kind == "dma":
                    key = ("dma",) + d.dsem
                    val = d.dtarget
                    sem = dsems[d.dsem[0]][d.dsem[1]]
                else:
                    key = ("eng", d.eng)
                    val = self.cum[d.eng][d.idx]
                    sem = esems[d.eng]
                if maxw.get(key, 0) >= val:
                    continue
                maxw[key] = val
                engine.wait_ge(sem, val)
            ins = o.fn(engine)
            if o.dma:
                ins.then_inc(dsems[o.dsem[0]][o.dsem[1]], 16)
            elif o.inc:
                ins.then_inc(esems[eng_name], 1)


def build_nc():
    nc = bass.Bass("TRN2", target_bir_lowering=False)
    S = Sched({"sp": 16, "pool": 8, "scr": 16})

    def din(name, shape, dt=F32):
        return nc.dram_tensor(name, list(shape), dt, kind="ExternalInput").ap()

    def dout(name, shape, dt=F32):
        return nc.dram_tensor(name, list(shape), dt, kind="ExternalOutput").ap()

    xp_all = din("xp_all", [SEQ, D])
    xp_own = din("xp_own", [NOWN * 128, D])
    xs_tok = din("xs_tok", [128, D])
    cckv = din("cckv", [4, SEQ, 128])
    ckr = din("ckr", [4, SEQ, 32])
    st_h = din("st_h", [128, 4, 4])
    st_conv = din("st_conv", [128, 4, 4, 3])
    w_in = din("w_in", [D, NIN])
    w_out = din("w_out", [D, D])
    w_up = din("w_up", [D, 4096])
    w_down = din("w_down", [4096, D])
    w_uq = din("w_uq", [256, 768])
    w_ukv = din("w_ukv", [128, 1024])
    w_a = din("w_a", [8, 64, 64])
    w_i = din("w_i", [8, 64, 64])
    vec = din("vec", [128, 4, 8])
    gpm = din("gpm", [128, 8])
    g_post_mix = din("g_post_mix", [128, D])
    g_pre_mlp = din("g_pre_mlp", [128, D])
    g_post_mlp = din("g_post_mlp", [128, D])
    g_q = din("g_q", [128, 256])
    g_kv = din("g_kv", [128, 128])
    ident_d = din("ident", [128, 128])
    maskb_d = din("maskb", [128, 2, 512])
    masks_d = din("masks", [128, 4, 256])
    sel_d = din("sel", [128, 2])
    tab_real = din("tab_real", [NB + 1, 128, 32])
    tab_own = din("tab_own", [NOWN, 128, 32])

    wup_bf = nc.dram_tensor("wup_bf", [16, 128, 8, 256], BF16, kind="Internal").ap()
    wdn_bf = nc.dram_tensor("wdn_bf", [2, 8, 128, 4, 512], BF16, kind="Internal").ap()

    y_own = dout("y_own", [NOWN * 128, D])
    y_s = dout("y_s", [128, D])
    ckv_p = dout("ckv_p", [SEQ, 128])
    kr_p = dout("kr_p", [SEQ, 32])
    h_p = dout("h_p", [128, 4])
    conv_p = dout("conv_p", [128, 4, 3])
    ckv_s = dout("ckv_s", [128, 128])
    kr_s = dout("kr_s", [128, 32])
    h_s = dout("h_s", [128, 4, 4])
    conv_s = dout("conv_s", [128, 4, 4, 3])

    sb_ptr = [(nc.sbuf_base + 63) // 64 * 64]
    sb_top = nc.sbuf_top

    def sb(name, shape, dt, at=None):
        nbytes = int(np.prod(shape[1:])) * (4 if dt == F32 else 2)
        nbytes = (nbytes + 63) // 64 * 64
        if at is None:
            off = sb_ptr[0]
            sb_ptr[0] += nbytes
            assert sb_ptr[0] <= sb_top, f"SBUF overflow at {name}: {sb_ptr[0]} > {sb_top}"
        else:
            off = at[0]
            at[0] += nbytes
            assert at[0] <= at[1], f"arena overflow at {name}: {at[0]} > {at[1]}"
        return nc.alloc_sbuf_tensor_at(name, list(shape), dt, offset=off)

    w_in_bf = sb("w_in_bf", [128, 8, NIN], BF16)
    w_out_lru = sb("w_out_lru", [128, 4, D], BF16)
    W_comb = sb("W_comb", [128, 8, D], BF16)
    W_ql = sb("W_ql", [128, 2, 8, 128], BF16)
    W_qr = sb("W_qr", [128, 2, 8, 32], BF16)
    Wa_bd = sb("Wa_bd", [128, 4, 128], BF16)
    Wi_bd = sb("Wi_bd", [128, 4, 128], BF16)
    gpostmix = sb("gpostmix", [128, D], F32)
    gpremlp = sb("gpremlp", [128, D], F32)
    gpostmlp = sb("gpostmlp", [128, D], F32)
    gq = sb("gq", [128, 256], F32)
    gkv = sb("gkv", [128, 128], F32)
    ident_bf = sb("ident_bf", [128, 128], BF16)
    ident_f = sb("ident_f", [128, 128], F32)
    ones_bf = sb("ones_bf", [128, 128], BF16)
    ones_f = sb("ones_f", [128, 128], F32)
    maskb = sb("maskb", [128, 2, 512], BF16)
    masks = sb("masks", [128, 4, 256], BF16)
    vecs = sb("vecs", [128, 4, 8], F32)
    sp8 = sb("sp8", [128, 4], F32)
    gpm_sb = sb("gpm_sb", [128, 8], F32)
    sel = sb("sel", [128, 2], F32)
    onecol = sb("onecol", [128, 1], F32)
    epscol = sb("epscol", [128, 1], F32)
    st_h_sb = sb("st_h_sb", [128, 4, 4], F32)
    hs_sb = sb("hs_sb", [128, 4, 4], F32)
    KTn = sb("KTn", [128, 128], BF16)
    krTn = sb("krTn", [32, 128], BF16)
    Vn = sb("Vn", [128, 128], BF16)
    KT = sb("KT", [128, SEQ], BF16)
    krT = sb("krT", [32, SEQ], BF16)
    V = sb("V", [128, NB, 128], BF16)
    x1 = sb("x1", [128, 4, D], F32)
    xn2T = sb("xn2T", [128, 8, 512], BF16)
    NWS = 2
    wup = [sb(f"wup{i}", [128, 8, 256], BF16) for i in range(NWS)]
    wdn_off = []
    wdn = []
    for i in range(NWS):
        wdn_off.append(sb_ptr[0])
        wdn.append(sb(f"wdn{i}", [128, 4, 512], BF16))
    qlatT2 = nc.alloc_sbuf_tensor_at("qlatT2", [128, 8, 128], BF16, offset=wdn_off[0])
    y_lruT2 = nc.alloc_sbuf_tensor_at("y_lruT2", [128, 4, 128], BF16, offset=wdn_off[0] + 2048)
    qrT2 = nc.alloc_sbuf_tensor_at("qrT2", [32, 8, 128], BF16, offset=wdn_off[1])
    xl_p_off = sb_ptr[0]
    xl_p = sb("xl_p", [128, 4, 3 + 256], F32)
    hstate = sb("hstate", [128, 4], F32)
    stat = sb("stat", [128, 64], F32)

    arena0 = sb_ptr[0]
    arena_end = sb_top

    aM = [arena0, arena_end]
    x_in0_off = aM[0]
    x_in = [sb(f"x_in{i}", [128, D], F32, aM) for i in range(2)]
    x_in1_off = x_in0_off + 4096
    xs = sb("xs", [128, D], BF16, aM)
    xnT = sb("xnT", [128, 8, 256], BF16, aM)
    xnT_own = sb("xnT_own", [128, 8, 128], BF16, aM)
    kv_f = sb("kv_f", [128, 160], F32, aM)
    kv_bf = sb("kv_bf", [128, 160], BF16, aM)
    tabr = [sb(f"tabr{i}", [128, 32], F32, aM) for i in range(2)]
    tabo = sb("tabo", [128, 32], F32, aM)
    rtmp = sb("rtmp", [128, 4, 16], F32, aM)
    xc = sb("xc", [128, 4, 256], F32, aM)
    xc_bf = sb("xc_bf", [128, 4, 256], BF16, aM)
    t_r = sb("t_r", [128, 4, 256], F32, aM)
    t_i = sb("t_i", [128, 4, 256], F32, aM)
    t_s = sb("t_s", [128, 256], F32, aM)
    hT = sb("hT", [128, 4, 256], F32, aM)
    hsel = sb("hsel", [128, 4, 128], F32, aM)
    g_a = sb("g_a", [128, 4, 128], F32, aM)
    g_b = sb("g_b", [128, 4, 128], F32, aM)
    cqn = sb("cqn", [128, 256], BF16, aM)
    cqnT = sb("cqnT", [128, 2, 128], BF16, aM)
    qr_r = sb("qr_r", [128, 8, 32], F32, aM)
    qr_s = sb("qr_s", [128, 8, 32], F32, aM)
    qr_bf = sb("qr_bf", [128, 8, 32], BF16, aM)
    PT = [sb(f"PT{i}", [128, 512], BF16, aM) for i in range(3)]
    Lacc = [sb(f"Lacc{i}", [128, 512], F32, aM) for i in range(2)]
    olatT = sb("olatT", [128, 8, 128], BF16, aM)
    mix_sb = sb("mix_sb", [128, D], F32, aM)
    xs2 = xs
    krc = nc.alloc_sbuf_tensor_at("krc", [128, NB, 32], BF16, offset=x_in1_off)
    xl_s = nc.alloc_sbuf_tensor_at("xl_s", [128, 4, 4, 35], F32, offset=xl_p_off)
    surv_off = aM[0]
    y_lruT = sb("y_lruT", [128, 4, 128], BF16, aM)
    qlatT = sb("qlatT", [128, 8, 128], BF16, aM)
    qrT = sb("qrT", [32, 8, 128], BF16, aM)
    aS = [arena0, arena_end]
    stg_in = [sb(f"stg_in{i}", [128, NIN], F32, aS) for i in range(2)]
    uq_f = sb("uq_f", [128, 2, 768], F32, aS)
    ukv_f = sb("ukv_f", [128, 1024], F32, aS)
    uqT = sb("uqT", [64, 2, 8, 128], BF16, aS)
    ukT = sb("ukT", [64, 8, 128], BF16, aS)
    uvT = sb("uvT", [64, 8, 128], BF16, aS)
    wo_mla = sb("wo_mla", [64, 8, D], BF16, aS)
    lam_e = sb("lam_e", [128, 4], F32, aS)
    aF = [arena0, arena_end]
    hid = sb("hid", [128, 32, 512], BF16, aF)
    ff_sb = sb("ff_sb", [128, 4, D], F32, aF)
    rl = [sb(f"rl{i}", [128, 512], F32, aF) for i in range(2)]
    assert aF[0] <= surv_off, f"F arena {aF[0]} overlaps survivors at {surv_off}"
    assert aS[0] <= surv_off

    banks = [nc.alloc_psum_tensor(f"bank{i}", [128, 512], F32) for i in range(8)]
    banks_bf = [b.bitcast(BF16) for b in banks]

    def BK(b, lo=0, hi=512):
        return [("bk", b)]

    def stage(k):
        if STAGE is not None and STAGE == k:
            raise _Stop()

    klog = {"keys": set()}

    def R(*ks):
        out = ["PHASE"]
        for k in ks:
            if isinstance(k, list):
                out.extend(k)
                klog["keys"].update(k)
            else:
                out.append(k)
                klog["keys"].add(k)
        return out

    def W(*ks):
        out = []
        for k in ks:
            if isinstance(k, list):
                out.extend(k)
                klog["keys"].update(k)
            else:
                out.append(k)
                klog["keys"].add(k)
        return out

    def phase_switch():
        keys = list(klog["keys"])
        S.op("dve", lambda e: e.memset(stat[:, 63:64], 0.0), reads=[], writes=keys + ["PHASE"])
        klog["keys"] = set()

    def fsz(ap):
        try:
            return int(ap.free_size)
        except Exception:
            return 256

    def dma(out_ap, in_ap, reads, writes, eng="sp", grp=None):
        return S.op(eng, lambda e: e.dma_start(out=out_ap, in_=in_ap), reads=reads, writes=writes, dma=True, grp=grp,
                    cost=2.5)

    def ecost(eng, out_ap):
        n_ = fsz(out_ap)
        if eng == "act":
            return 0.25 + n_ / 1400.0
        if eng == "pool":
            return 0.15 + n_ / 450.0
        return 0.08 + n_ / 960.0

    def mm(out_ap, lhsT, rhs, start, stop, reads, writes, noatom=False):
        atom = None
        if noatom:
            pass
        elif start and not stop:
            atom = "begin"
        elif not start:
            atom = "end" if stop else "mid"
        return S.op("pe", lambda e: e.matmul(out_ap, lhsT, rhs, start=start, stop=stop), reads=reads, writes=writes,
                    atom=atom, cost=max(0.06, fsz(out_ap) / 1200.0))

    def tr(out_ap, in_ap, ident_ap, reads, writes):
        return S.op("pe", lambda e: e.transpose(out_ap, in_ap, ident_ap), reads=reads, writes=writes, cost=0.13)

    def act(out_ap, in_ap, func, reads, writes, bias=None, scale=None, accum_out=None):
        kw = {}
        if bias is not None:
            kw["bias"] = bias
        if scale is not None:
            kw["scale"] = scale
        if accum_out is not None:
            kw["accum_out"] = accum_out
        return S.op("act", lambda e: e.activation(out_ap, in_ap, func, **kw), reads=reads, writes=writes,
                    cost=0.25 + fsz(out_ap) / 1400.0)

    def ts(eng, out_ap, in0, s1, s2, op0, op1, reads, writes):
        c_ = ecost(eng, out_ap)
        if op1 is None:
            return S.op(eng, lambda e: e.tensor_scalar(out_ap, in0, s1, None, op0), reads=reads, writes=writes, cost=c_)
        return S.op(eng, lambda e: e.tensor_scalar(out_ap, in0, s1, s2, op0, op1), reads=reads, writes=writes, cost=c_)

    def tt(eng, out_ap, in0, in1, op, reads, writes):
        return S.op(eng, lambda e: e.tensor_tensor(out_ap, in0, in1, op), reads=reads, writes=writes, cost=ecost(eng, out_ap))

    def stt(out_ap, in0, scalar, in1, op0, op1, reads, writes):
        return S.op("dve", lambda e: e.scalar_tensor_tensor(out_ap, in0, scalar, in1, op0, op1),
                    reads=reads, writes=writes, cost=ecost("dve", out_ap))

    def cp(eng, out_ap, in_ap, reads, writes):
        if eng == "act":
            return S.op("act", lambda e: e.copy(out_ap, in_ap), reads=reads, writes=writes, cost=ecost("act", out_ap))
        return S.op(eng, lambda e: e.tensor_copy(out_ap, in_ap), reads=reads, writes=writes, cost=ecost(eng, out_ap))

    scol = [0]

    def newstat(n=1):
        c = scol[0]
        if c + n > 60:
            c = 0
        scol[0] = c + n
        return c

    def rstd_from(cin, n, cout):
        act(stat[:, cout:cout + 1], stat[:, cin:cin + 1], AF.Ln, reads=R(("stat", cin), "epscol"), writes=W(("stat", cout)),
            scale=1.0 / n, bias=epscol[:, 0:1])
        act(stat[:, cout:cout + 1], stat[:, cout:cout + 1], AF.Exp, reads=R(("stat", cout)), writes=W(("stat", cout)),
            scale=-0.5)

    dma(vecs[:], vec, R(), W("vecs"))
    dma(gpm_sb[:], gpm, R(), W("gpm"))
    dma(sel[:], sel_d, R(), W("sel"))
    dma(gpostmix[:], g_post_mix, R(), W("gpostmix"))
    dma(gpremlp[:], g_pre_mlp, R(), W("gpremlp"))
    dma(gpostmlp[:], g_post_mlp, R(), W("gpostmlp"))
    dma(gq[:], g_q, R(), W("gq"))
    dma(gkv[:], g_kv, R(), W("gkv"))
    dma(ident_f[:], ident_d, R(), W("ident_f"))
    dma(ident_bf[:], ident_d, R(), W("ident_bf"), eng="pool")
    dma(maskb[:], maskb_d, R(), W("maskb"), eng="pool")
    dma(masks[:], masks_d, R(), W("masks"), eng="pool")
    S.op("dve", lambda e: e.memset(ones_bf[:], 1.0), reads=R(), writes=W("ones"))
    S.op("dve", lambda e: e.memset(ones_f[:], 1.0), reads=R(), writes=W("ones_f"))
    S.op("dve", lambda e: e.memset(onecol[:], 1.0), reads=R(), writes=W("onecol"))
    S.op("dve", lambda e: e.memset(epscol[:], EPS), reads=R(), writes=W("epscol"))
    S.op("dve", lambda e: e.memset(hstate[:], 0.0), reads=R(), writes=W(*[("hstate", ct) for ct in range(4)]))
    S.op("pool", lambda e: e.memset(Wa_bd[:], 0.0), reads=R(), writes=W("Wa_bd"))
    S.op("pool", lambda e: e.memset(Wi_bd[:], 0.0), reads=R(), writes=W("Wi_bd"))
    for hd in range(8):
        r0 = (hd % 2) * 64
        dma(Wa_bd[r0:r0 + 64, hd // 2, r0:r0 + 64], w_a[hd], R(), W("Wa_bd"), eng="pool")
        dma(Wi_bd[r0:r0 + 64, hd // 2, r0:r0 + 64], w_i[hd], R(), W("Wi_bd"), eng="pool")
    dma(w_out_lru[:], w_out[0:512, :].rearrange("(c p) n -> p c n", p=128), R(), W("w_out_lru"), eng="pool")
    dma(wo_mla[:], w_out[512:1024, :].rearrange("(h v) n -> v h n", v=64), R(), W("wo_mla"), eng="pool")
    w_in_v = w_in.rearrange("(c p) n -> c p n", p=128)
    for c in range(8):
        sl = c % 2
        dma(stg_in[sl][:], w_in_v[c], R(), W(("stg_in", sl)))
        ts("dve", w_in_bf[:, c, :], stg_in[sl][:], gpm_sb[:, c:c + 1], None, ALU.mult, None,
           reads=R(("stg_in", sl), "gpm"), writes=W("w_in_bf"))
    act(lam_e[:], vecs[:, :, 7], AF.Exp, reads=R("vecs"), writes=W("lam_e"), scale=-1.0)
    act(lam_e[:], lam_e[:], AF.Ln, reads=R("lam_e", "onecol"), writes=W("lam_e"), bias=onecol[:, 0:1])
    ts("dve", sp8[:], lam_e[:], -8.0, None, ALU.mult, None, reads=R("lam_e"), writes=W("sp8"))
    dma(uq_f[:], w_uq.rearrange("(c p) n -> p c n", p=128), R(), W("uq_f"))
    dma(ukv_f[:], w_ukv, R(), W("ukv_f"))
    uq4 = uq_f[:].rearrange("p c (h e) -> p c h e", e=96)
    ukv3 = ukv_f[:].rearrange("p (h e) -> p h e", e=128)
    for c in range(2):
        cp("dve", W_qr[:, c, :, :], uq4[:, c, :, 64:96], reads=R("uq_f"), writes=W("W_qr"))
    tcount = 0
    for c in range(2):
        for h in range(8):
            bk = 6 + (tcount % 2)
            tcount += 1
            tr(banks[bk][0:64, 0:128], uq4[:, c, h, 0:64], ident_f[:], reads=R("uq_f", "ident_f"), writes=W(BK(bk)))
            cp("dve" if h % 2 == 0 else "act", uqT[:, c, h, :], banks[bk][0:64, 0:128], reads=R(BK(bk)), writes=W("uqT"))
    for h in range(8):
        bk = 6 + (tcount % 2)
        tcount += 1
        tr(banks[bk][0:64, 0:128], ukv3[:, h, 0:64], ident_f[:], reads=R("ukv_f", "ident_f"), writes=W(BK(bk)))
        cp("dve", ukT[:, h, :], banks[bk][0:64, 0:128], reads=R(BK(bk)), writes=W("ukT"))
        bk = 6 + (tcount % 2)
        tcount += 1
        tr(banks[bk][0:64, 0:128], ukv3[:, h, 64:128], ident_f[:], reads=R("ukv_f", "ident_f"), writes=W(BK(bk)))
        cp("act", uvT[:, h, :], banks[bk][0:64, 0:128], reads=R(BK(bk)), writes=W("uvT"))
    for c in range(2):
        for h in range(8):
            bk = 4 + (tcount % 2)
            tcount += 1
            mm(banks[bk][:, 0:128], uqT[:, c, h, :], ukT[:, h, :], True, True, reads=R("uqT", "ukT"), writes=W(BK(bk)))
            cp("dve" if h % 2 == 0 else "act", W_ql[:, c, h, :], banks[bk][:, 0:128], reads=R(BK(bk)), writes=W("W_ql"))
    for h in range(8):
        for dh in range(2):
            bk = 4 + (tcount % 2)
            tcount += 1
            mm(banks[bk][:, :], uvT[:, h, :], wo_mla[:, h, dh * 512:(dh + 1) * 512], True, True,
               reads=R("uvT", "wo_mla"), writes=W(BK(bk)))
            cp("dve" if dh == 0 else "act", W_comb[:, h, dh * 512:(dh + 1) * 512], banks[bk][:, :],
               reads=R(BK(bk)), writes=W("W_comb"))
    scratch_jobs = []
    for c in range(8):
        scratch_jobs.append((wup_bf[:, :, c, :], w_up[c * 128:(c + 1) * 128, :].rearrange("p (g f) -> g p f", f=256), "wup_bf"))
    for fg_ in range(8):
        for dh_ in range(2):
            scratch_jobs.append((wdn_bf[dh_, fg_],
                                 w_down[fg_ * 512:(fg_ + 1) * 512, dh_ * 512:(dh_ + 1) * 512].rearrange("(fl p) d -> p fl d", p=128),
                                 "wdn_bf"))

    def scratch_pump(n=1):
        for _ in range(n):
            if scratch_jobs:
                o_, i_, k_ = scratch_jobs.pop(0)
                dma(o_, i_, [], [k_], eng="pool", grp="scr")

    phase_switch()

    xcnt = [0]
    tcnt = [0]

    xpre = [False]

    def norm_transpose(src_rows, col0, next_rows=None):
        sl = xcnt[0] % 2
        xcnt[0] += 1
        xt = x_in[sl]
        kx = ("x_in", sl)
        if not xpre[0]:
            dma(xt[:], src_rows, R(), W(kx))
        xpre[0] = False
        if next_rows is not None:
            nsl = xcnt[0] % 2
            dma(x_in[nsl][:], next_rows, R(), W(("x_in", nsl)))
            xpre[0] = True
        c = newstat(2)
        act(xs[:], xt[:], AF.Square, reads=R(kx), writes=W("xs", ("stat", c)), accum_out=stat[:, c:c + 1])
        rstd_from(c, D, c + 1)
        act(xs[:], xt[:], AF.Copy, reads=R(kx, ("stat", c + 1)), writes=W("xs"), scale=stat[:, c + 1:c + 2])
        for ch in range(8):
            tr(banks_bf[0][:, ch * 128:(ch + 1) * 128], xs[:, ch * 128:(ch + 1) * 128], ident_bf[:],
               reads=R("xs", "ident_bf"), writes=W(BK(0)))
        cp("dve", xnT[:, :, col0:col0 + 128], banks_bf[0][:, :].rearrange("p (c t) -> p c t", c=8),
           reads=R(BK(0)), writes=W("xnT"))

    def kv_block(col0, tab_idx, out_ckv_rows, out_kr_rows, Vd, KTd, krTd, kV, kKT, kkrT):
        kb1 = BK(4)
        for ch in range(8):
            mm(banks[4][:, 0:160], xnT[:, ch, col0:col0 + 128], w_in_bf[:, ch, 1280:1440], ch == 0, ch == 7,
               reads=R("xnT", "w_in_bf"), writes=W(kb1))
        sl = tcnt[0] % 2
        tcnt[0] += 1
        dma(tabr[sl][:], tab_real[tab_idx], R(), W(("tabr", sl)))
        c = newstat(2)
        act(kv_bf[:, 0:128], banks[4][:, 0:128], AF.Square, reads=R(kb1), writes=W("kv_bf", ("stat", c)),
            accum_out=stat[:, c:c + 1])
        rstd_from(c, 128, c + 1)
        stt(kv_f[:, 0:128], banks[4][:, 0:128], stat[:, c + 1:c + 2], gkv[:], ALU.mult, ALU.mult,
            reads=R(kb1, ("stat", c + 1), "gkv"), writes=W("kv_f"))
        cosv = tabr[sl][:, 0:16]
        sinv = tabr[sl][:, 16:32]
        kx1 = banks[4][:, 128:144]
        kx2 = banks[4][:, 144:160]
        rk = R(kb1, ("tabr", sl))
        tt("dve", rtmp[:, 0, :], kx1, cosv, ALU.mult, reads=rk, writes=W("rtmp"))
        tt("dve", rtmp[:, 1, :], kx2, sinv, ALU.mult, reads=rk, writes=W("rtmp"))
        tt("dve", rtmp[:, 2, :], kx2, cosv, ALU.mult, reads=rk, writes=W("rtmp"))
        tt("dve", rtmp[:, 3, :], kx1, sinv, ALU.mult, reads=rk, writes=W("rtmp"))
        tt("dve", kv_f[:, 128:144], rtmp[:, 0, :], rtmp[:, 1, :], ALU.subtract, reads=R("rtmp"), writes=W("kv_f"))
        tt("dve", kv_f[:, 144:160], rtmp[:, 2, :], rtmp[:, 3, :], ALU.add, reads=R("rtmp"), writes=W("kv_f"))
        dma(out_ckv_rows, kv_f[:, 0:128], R("kv_f"), W())
        dma(out_kr_rows, kv_f[:, 128:160], R("kv_f"), W())
        cp("dve", Vd, kv_f[:, 0:128], reads=R("kv_f"), writes=W(kV))
        cp("dve", kv_bf[:, 128:160], kv_f[:, 128:160], reads=R("kv_f"), writes=W("kv_bf"))
        tr(banks_bf[5][:, 0:128], Vd, ident_bf[:], reads=R(kV, "ident_bf"), writes=W(BK(5)))
        tr(banks_bf[5][0:32, 128:256], kv_bf[:, 128:160], ident_bf[:], reads=R("kv_bf", "ident_bf"), writes=W(BK(5)))
        cp("act", KTd, banks_bf[5][:, 0:128], reads=R(BK(5)), writes=W(kKT))
        cp("act", krTd, banks_bf[5][0:32, 128:256], reads=R(BK(5)), writes=W(kkrT))

    def lru(nreal, sample):
        for ct in range(4):
            bk = 4 + ct % 2
            off = 0
            kb_ = BK(bk)
            for ch in range(8):
                mm(banks[bk][:, off:off + nreal], w_in_bf[:, ch, ct * 128:(ct + 1) * 128], xnT[:, ch, 0:nreal],
                   ch == 0, ch == 7, reads=R("xnT", "w_in_bf"), writes=W(kb_))
            if not sample:
                cp("act", xl_p[:, ct, 3:3 + nreal], banks[bk][:, off:off + nreal], reads=R(kb_), writes=W(("xl", ct)))
            else:
                for s_ in range(4):
                    cp("act", xl_s[:, ct, s_, 3:35], banks[bk][:, off + s_ * 32:off + (s_ + 1) * 32],
                       reads=R(kb_), writes=W(("xl", ct)))
        stage(231)
        for ct in range(4):
            kxl = ("xl", ct)
            kxc = ("xc", ct)
            if not sample:
                src = [xl_p[:, ct, k:k + nreal] for k in range(4)]
                dst = xc[:, ct, 0:nreal]
            else:
                src = [xl_s[:, ct, :, k:k + 32] for k in range(4)]
                dst = xc[:, ct, 0:128].rearrange("p (s t) -> p s t", s=4)
            ts("dve", dst, src[3], vecs[:, ct, 3:4], vecs[:, ct, 4:5], ALU.mult, ALU.add, reads=R(kxl, "vecs"), writes=W(kxc))
            for k in (2, 1, 0):
                stt(dst, src[k], vecs[:, ct, k:k + 1], dst, ALU.mult, ALU.add, reads=R(kxl, kxc, "vecs"), writes=W(kxc))
            cp("act", xc_bf[:, ct, 0:nreal], xc[:, ct, 0:nreal], reads=R(kxc), writes=W(("xc_bf", ct)))
        stage(232)
        n_ = nreal
        for ct in range(4):
            bk = 4 + ct % 2
            kg = BK(bk)
            mm(banks[bk][:, 0:n_], Wa_bd[:, ct, :], xc_bf[:, ct, 0:n_], True, True,
               reads=R(("xc_bf", ct), "Wa_bd"), writes=W(kg))
            act(t_r[:, ct, 0:n_], banks[bk][:, 0:n_], AF.Sigmoid, reads=R(kg, "vecs"), writes=W(("t_r", ct)),
                bias=vecs[:, ct, 5:6])
            bk2 = 5 - ct % 2
            kg2 = BK(bk2)
            mm(banks[bk2][:, 0:n_], Wi_bd[:, ct, :], xc_bf[:, ct, 0:n_], True, True,
               reads=R(("xc_bf", ct), "Wi_bd"), writes=W(kg2))
            act(t_i[:, ct, 0:n_], banks[bk2][:, 0:n_], AF.Sigmoid, reads=R(kg2, "vecs"), writes=W(("t_i", ct)),
                bias=vecs[:, ct, 6:7])
        for ct in range(4):
            tt("dve", t_i[:, ct, 0:n_], t_i[:, ct, 0:n_], xc[:, ct, 0:n_], ALU.mult, reads=R(("t_i", ct), ("xc", ct)),
               writes=W(("t_i", ct)))
        TA = [t_r[:, ct, 0:n_] for ct in range(4)]
        TB = [t_i[:, ct, 0:n_] for ct in range(4)]
        TS = [xc[:, ct, 0:n_] for ct in range(4)]
        for ct in range(4):
            act(TA[ct], TA[ct], AF.Exp, reads=R(("t_r", ct), "sp8"), writes=W(("t_r", ct)), scale=sp8[:, ct:ct + 1])
        for ct in range(4):
            tt("dve", TS[ct], TA[ct], TA[ct], ALU.mult, reads=R(("t_r", ct), ("t_i", ct)), writes=W(("xc", ct)))
        for ct in range(4):
            act(TS[ct], TS[ct], AF.Ln, reads=R(("xc", ct), "onecol"), writes=W(("xc", ct)), scale=-1.0, bias=onecol[:, 0:1])
        for ct in range(4):
            act(TS[ct], TS[ct], AF.Exp, reads=R(("xc", ct)), writes=W(("xc", ct)), scale=0.5)
        for ct in range(4):
            tt("dve", TB[ct], TB[ct], TS[ct], ALU.mult, reads=R(("t_i", ct), ("xc", ct)), writes=W(("t_i", ct)))
        for ct in range(4):
            stage(236)
            if not sample:
                S.op("dve", lambda e, ct=ct: e.tensor_tensor_scan(hT[:, ct, 0:nreal], t_r[:, ct, 0:nreal], t_i[:, ct, 0:nreal],
                                                                   hstate[:, ct:ct + 1], ALU.mult, ALU.add),
                     reads=R(("t_r", ct), ("t_i", ct), ("hstate", ct)), writes=W(("hT", ct)), cost=0.65)
                cp("dve", hstate[:, ct:ct + 1], hT[:, ct, nreal - 1:nreal], reads=R(("hT", ct)), writes=W(("hstate", ct)))
                cp("pool", xl_p[:, ct, 0:3], xl_p[:, ct, nreal:nreal + 3], reads=R(("xl", ct)), writes=W(("xl", ct)))
            else:
                a3 = t_r[:, ct, 0:128].rearrange("p (s t) -> p s t", s=4)[:, :, 0]
                b3 = t_i[:, ct, 0:128].rearrange("p (s t) -> p s t", s=4)[:, :, 0]
                tt("dve", rtmp[:, 0, 0:4], a3, st_h_sb[:, ct, :], ALU.mult, reads=R(("t_r", ct), "st_h"), writes=W("rtmp"))
                tt("dve", b3, b3, rtmp[:, 0, 0:4], ALU.add, reads=R(("t_i", ct), "rtmp"), writes=W(("t_i", ct)))
                S.op("dve", lambda e, a3=a3: e.memset(a3, 0.0), reads=R("rtmp"), writes=W(("t_r", ct)))
                S.op("dve", lambda e, ct=ct: e.tensor_tensor_scan(hT[:, ct, 0:128], t_r[:, ct, 0:128], t_i[:, ct, 0:128],
                                                                   0.0, ALU.mult, ALU.add),
                     reads=R(("t_r", ct), ("t_i", ct)), writes=W(("hT", ct)), cost=0.4)

    def owned_front_q(tab_ap, BS, cqb, cql, qlb):
        kq = BK(cqb)
        for ch in range(8):
            mm(banks[cqb][:, cql:cql + 256], xnT_own[:, ch, :], w_in_bf[:, ch, 1024:1280], ch == 0, ch == 7,
               reads=R("xnT_own", "w_in_bf"), writes=W(kq))
        c = newstat(2)
        act(cqn[:], banks[cqb][:, cql:cql + 256], AF.Square, reads=R(kq), writes=W("cqn", ("stat", c)),
            accum_out=stat[:, c:c + 1])
        rstd_from(c, 256, c + 1)
        stt(cqn[:], banks[cqb][:, cql:cql + 256], stat[:, c + 1:c + 2], gq[:], ALU.mult, ALU.mult,
            reads=R(kq, ("stat", c + 1), "gq"), writes=W("cqn"))
        for cc in range(2):
            tr(banks_bf[0][:, cc * 128:(cc + 1) * 128], cqn[:, cc * 128:(cc + 1) * 128], ident_bf[:],
               reads=R("cqn", "ident_bf"), writes=W(BK(0)))
        cp("dve", cqnT[:], banks_bf[0][:, 0:256].rearrange("p (c t) -> p c t", c=2), reads=R(BK(0)), writes=W("cqnT"))
        for h in range(8):
            bk = qlb[h // 4]
            off = (h % 4) * 128
            for cc in range(2):
                mm(banks[bk][:, off:off + 128], W_ql[:, cc, h, :], cqnT[:, cc, :], cc == 0, cc == 1,
                   reads=R("W_ql", "cqnT"), writes=W(BK(bk, off, off + 128)))
        for bi, bk in enumerate(qlb):
            cp("act", BS["ql"][:, bi * 4:bi * 4 + 4, :], banks[bk][:, :].rearrange("p (h t) -> p h t", h=4),
               reads=R(BK(bk)), writes=W(BS["kql"]))
        for cc in range(2):
            mm(banks[cqb][:, cql:cql + 256], cqnT[:, cc, :], W_qr[:, cc, :, :].rearrange("p h e -> p (h e)"), cc == 0, cc == 1,
               reads=R("cqnT", "W_qr"), writes=W(kq))
        q3 = banks[cqb][:, cql:cql + 256].rearrange("p (h e) -> p h e", e=32)
        cosb = tab_ap[:, 0:16].unsqueeze(1).to_broadcast([128, 8, 16])
        sinb = tab_ap[:, 16:32].unsqueeze(1).to_broadcast([128, 8, 16])
        tt("dve", qr_r[:, :, 0:16], q3[:, :, 0:16], cosb, ALU.mult, reads=R(kq, "tabo"), writes=W("qr_r"))
        tt("dve", qr_r[:, :, 16:32], q3[:, :, 16:32], cosb, ALU.mult, reads=R(kq, "tabo"), writes=W("qr_r"))
        tt("dve", qr_s[:, :, 0:16], q3[:, :, 0:16], sinb, ALU.mult, reads=R(kq, "tabo"), writes=W("qr_s"))
        tt("dve", qr_s[:, :, 16:32], q3[:, :, 16:32], sinb, ALU.mult, reads=R(kq, "tabo"), writes=W("qr_s"))
        tt("dve", qr_bf[:, :, 0:16], qr_r[:, :, 0:16], qr_s[:, :, 16:32], ALU.subtract,
           reads=R("qr_r", "qr_s"), writes=W("qr_bf"))
        tt("dve", qr_bf[:, :, 16:32], qr_r[:, :, 16:32], qr_s[:, :, 0:16], ALU.add,
           reads=R("qr_r", "qr_s"), writes=W("qr_bf"))
        for h in range(8):
            tr(banks_bf[0][0:32, h * 128:(h + 1) * 128], qr_bf[:, h, :], ident_bf[:], reads=R("qr_bf", "ident_bf"),
               writes=W(BK(0)))
        cp("dve", BS["qr"][:], banks_bf[0][0:32, :].rearrange("p (h t) -> p h t", h=8), reads=R(BK(0)), writes=W(BS["kqr"]))

    def owned_front_bg(tab_ap, BS, gb, cqb, cql, qlb):
        for ct in range(4):
            bk = gb[ct]
            kg = BK(bk)
            for ch in range(8):
                mm(banks[bk][:, 0:128], w_in_bf[:, ch, 512 + ct * 128:512 + (ct + 1) * 128], xnT_own[:, ch, :],
                   ch == 0, ch == 7, reads=R("xnT_own", "w_in_bf"), writes=W(kg))
            g = banks[bk][:, 0:128]
            act(g_a[:, ct, :], g, AF.Square, reads=R(kg), writes=W(("g_a", ct)))
            cp("act", g_b[:, ct, :], g, reads=R(kg), writes=W(("g_b", ct)))
            ts("dve", g_a[:, ct, :], g_a[:, ct, :], 0.044715, 1.0, ALU.mult, ALU.add, reads=R(("g_a", ct)), writes=W(("g_a", ct)))
            tt("dve", g_a[:, ct, :], g_a[:, ct, :], g_b[:, ct, :], ALU.mult, reads=R(("g_a", ct), ("g_b", ct)), writes=W(("g_a", ct)))
        for ct in range(4):
            act(g_a[:, ct, :], g_a[:, ct, :], AF.Sigmoid, reads=R(("g_a", ct)), writes=W(("g_a", ct)), scale=GELU_C)
        for ct in range(4):
            tt("dve", g_b[:, ct, :], g_b[:, ct, :], g_a[:, ct, :], ALU.mult, reads=R(("g_b", ct), ("g_a", ct)), writes=W(("g_b", ct)))
            tt("dve", BS["y"][:, ct, :], g_b[:, ct, :], hsel[:, ct, :], ALU.mult, reads=R(("g_b", ct), ("hsel", ct)),
               writes=W(BS["ky"]))
        owned_front_q(tab_ap, BS, cqb, cql, qlb)

    def owned_front(tab_ap, BS, bgm):
        gb = [4, 5, 4, 5] if bgm else [4, 5, 6, 7]
        cqb, cql = (5, 0) if bgm else (1, 160)
        qlb = [4, 5] if bgm else [2, 3]
        if bgm:
            return owned_front_bg(tab_ap, BS, gb, cqb, cql, qlb)
        for ct in range(4):
            bk = gb[ct]
            kg = BK(bk)
            for ch in range(8):
                mm(banks[bk][:, 0:128], w_in_bf[:, ch, 512 + ct * 128:512 + (ct + 1) * 128], xnT_own[:, ch, :],
                   ch == 0, ch == 7, reads=R("xnT_own", "w_in_bf"), writes=W(kg))
        for ct in range(4):
            g = banks[gb[ct]][:, 0:128]
            kg = BK(gb[ct])
            act(g_a[:, ct, :], g, AF.Square, reads=R(kg), writes=W(("g_a", ct)))
            ts("dve", g_a[:, ct, :], g_a[:, ct, :], 0.044715, 1.0, ALU.mult, ALU.add, reads=R(("g_a", ct)), writes=W(("g_a", ct)))
            tt("dve", g_a[:, ct, :], g_a[:, ct, :], g, ALU.mult, reads=R(("g_a", ct), kg), writes=W(("g_a", ct)))
        for ct in range(4):
            act(g_b[:, ct, :], g_a[:, ct, :], AF.Sigmoid, reads=R(("g_a", ct)), writes=W(("g_b", ct)), scale=GELU_C)
        for ct in range(4):
            g = banks[gb[ct]][:, 0:128]
            kg = BK(gb[ct])
            tt("dve", g_b[:, ct, :], g_b[:, ct, :], g, ALU.mult, reads=R(("g_b", ct), kg), writes=W(("g_b", ct)))
            tt("dve", BS["y"][:, ct, :], g_b[:, ct, :], hsel[:, ct, :], ALU.mult, reads=R(("g_b", ct), ("hsel", ct)),
               writes=W(BS["ky"]))
        owned_front_q(tab_ap, BS, cqb, cql, qlb)

    sbank = [0]
    ptc = [0]

    SBANKS = [6, 7, 1]
    HOP = 0.6

    def sched_merge(F, B, constrained=False):
        if not B:
            return list(F)
        if not F:
            return list(B)
        eng_free = {}
        wfin, rfin = {}, {}
        from collections import Counter
        remR, remW = Counter(), Counter()

        def ksets(item):
            ops_ = item if isinstance(item, list) else [item]
            r_, w_ = set(), set()
            for o_ in ops_:
                r_.update(o_[2])
                w_.update(o_[3])
            r_.discard("PHASE")
            return r_, w_

        kF = [ksets(a) for a in F] if constrained else None
        kB = [ksets(b) for b in B] if constrained else None
        if constrained:
            for r_, w_ in kF:
                remR.update(r_)
                remW.update(w_)

        def start_of(item, commit):
            ops_ = item if isinstance(item, list) else [item]
            local_free = eng_free if commit else dict(eng_free)
            lw = wfin if commit else {}
            lr = rfin if commit else {}

            def getw(k):
                v = lw.get(k)
                return v if v is not None else wfin.get(k)

            def getr(k):
                v = lr.get(k)
                return v if v is not None else rfin.get(k)

            first = None
            for (eng, _f, reads_, writes_, dma_, _g, cost_) in ops_:
                rdy = 0.0
                for k in reads_:
                    if k == "PHASE":
                        continue
                    v = getw(k)
                    if v is not None:
                        rdy = max(rdy, v[0] + (HOP if v[1] != eng else 0.0))
                for k in writes_:
                    v = getw(k)
                    if v is not None:
                        rdy = max(rdy, v[0] + (HOP if v[1] != eng else 0.0))
                    v = getr(k)
                    if v is not None:
                        rdy = max(rdy, v[0] + (HOP if v[1] != eng else 0.0))
                st = max(local_free.get(eng, 0.0), rdy)
                if first is None:
                    first = st
                fin = st + cost_
                local_free[eng] = st + (0.15 if dma_ else cost_)
                for k in reads_:
                    if k != "PHASE":
                        o_ = getr(k)
                        if o_ is None or o_[0] < fin:
                            lr[k] = (fin, eng if not dma_ else "dma")
                for k in writes_:
                    lw[k] = (fin, eng if not dma_ else "dma")
            return first

        out = []
        i = j = 0
        while i < len(F) or j < len(B):
            if i >= len(F):
                pick = "b"
            elif j >= len(B):
                pick = "f"
            else:
                ok = True
                if constrained:
                    r_, w_ = kB[j]
                    ok = not (any(remR[k] > 0 or remW[k] > 0 for k in w_) or any(remW[k] > 0 for k in r_))
                if not ok:
                    pick = "f"
                else:
                    sf = start_of(F[i], False)
                    sb_ = start_of(B[j], False)
                    pick = "f" if sf <= sb_ else "b"
            if pick == "f":
                start_of(F[i], True)
                out.append(F[i])
                if constrained:
                    remR.subtract(kF[i][0])
                    remW.subtract(kF[i][1])
                i += 1
            else:
                start_of(B[j], True)
                out.append(B[j])
                j += 1
        return out

    def attn(groups, kblocks, bg=None, constrained=False):
        nk = len(kblocks)
        units = [(ki, kb, gi, g) for ki, kb in enumerate(kblocks) for gi, g in enumerate(groups)]
        LA = 2
        pend = []

        def issue_S(ki, kb, gi, g):
            N = g["N"]
            bk = SBANKS[sbank[0] % 3]
            sbank[0] += 1
            st = banks[bk][:, 0:N]
            kst = BK(bk)
            mk = kb["masks"][gi] if kb["masks"] is not None else None
            mm(st, kb["KT"], g["ql"], True, False, reads=R(g["kql"], *kb["keys"]), writes=W(kst))
            mm(st, kb["krT"], g["qr"], False, mk is None, reads=R(g["kqr"], *kb["keys"]), writes=W(kst))
            if mk is not None:
                mm(st, ident_bf[:], mk, False, True, reads=R("ident_bf", "maskb", "masks"), writes=W(kst))
            pi = ptc[0] % 3
            ptc[0] += 1
            pt = PT[pi][:, 0:N]
            act(pt, st, AF.Exp, reads=R(kst), writes=W(("PT", pi)), scale=ATTN_SCALE)
            return (ki, kb, g, pt, pi)

        def issue_PV(ki, kb, g, pt, pi):
            N = g["N"]
            mm(g["O"], kb["V"], pt, ki == 0, ki == nk - 1, reads=R(("PT", pi), *kb["keys"]), writes=W(g["kO"]), noatom=True)
            la = Lacc[g["li"]][:, 0:N]
            kla = ("Lacc", g["li"])
            le = "pool" if (g["li"] == 1 and LACC_POOL) else "dve"
            if ki == 0:
                cp(le, la, pt, reads=R(("PT", pi)), writes=W(kla))
            else:
                tt(le, la, la, pt, ALU.add, reads=R(("PT", pi), kla), writes=W(kla))

        prev_def = S.defer
        S.defer = []
        for u in units:
            pend.append(issue_S(*u))
            if len(pend) > LA:
                issue_PV(*pend.pop(0))
        while pend:
            issue_PV(*pend.pop(0))
        fg_list = S.defer
        S.defer = prev_def
        merged = sched_merge(fg_list, bg if bg else [], constrained=constrained)
        if bg:
            bg[:] = []
        S.flush(merged, len(merged))
        for g in groups:
            N = g["N"]
            kla = ("Lacc", g["li"])
            rv = Lacc[g["li"]][:, 0:N]
            mm(g["L"], ones_f[:], rv, True, True, reads=R(kla, "ones_f"), writes=W(g["kL"]))
            act(rv, g["L"], AF.Ln, reads=R(g["kL"]), writes=W(kla))
            act(rv, rv, AF.Exp, reads=R(kla), writes=W(kla), scale=-1.0)
            g["fin"](rv, kla)

    def mix_and_prep(x_rows, slot, BS, bgm=False):
        if x_rows is not None:
            dma(x1[:, slot, :], x_rows, R(), W(("x1", slot)))
        c = newstat(4)
        mb = [4, 5] if bgm else [6, 7]
        for dh in range(2):
            bk = mb[dh]
            kk = BK(bk)
            for ct in range(4):
                mm(banks[bk][:, :], BS["y"][:, ct, :], w_out_lru[:, ct, dh * 512:(dh + 1) * 512], ct == 0, False,
                   reads=R(BS["ky"], "w_out_lru"), writes=W(kk))
            for h in range(8):
                mm(banks[bk][:, :], olatT[:, h, :], W_comb[:, h, dh * 512:(dh + 1) * 512], False, h == 7,
                   reads=R("olatT", "W_comb"), writes=W(kk))
            act(mix_sb[:, dh * 512:(dh + 1) * 512], banks[bk][:, :], AF.Square, reads=R(kk),
                writes=W("mix_sb", ("stat", c + dh)), accum_out=stat[:, c + dh:c + dh + 1])
        tt("dve", stat[:, c + 2:c + 3], stat[:, c:c + 1], stat[:, c + 1:c + 2], ALU.add,
           reads=R(("stat", c), ("stat", c + 1)), writes=W(("stat", c + 2)))
        rstd_from(c + 2, D, c + 3)
        for dh in range(2):
            bk = mb[dh]
            stt(mix_sb[:, dh * 512:(dh + 1) * 512], banks[bk][:, :], stat[:, c + 3:c + 4],
                gpostmix[:, dh * 512:(dh + 1) * 512], ALU.mult, ALU.mult,
                reads=R(BK(bk), ("stat", c + 3), "gpostmix"), writes=W("mix_sb"))
        tt("dve", x1[:, slot, :], x1[:, slot, :], mix_sb[:], ALU.add, reads=R(("x1", slot), "mix_sb"), writes=W(("x1", slot)))
        c2 = newstat(2)
        msb = mix_sb[:].bitcast(BF16)
        xs2v = msb[:, 0:1024]
        act(msb[:, 1024:2048], x1[:, slot, :], AF.Square, reads=R(("x1", slot), "mix_sb"), writes=W("mix_sb", ("stat", c2)),
            accum_out=stat[:, c2:c2 + 1])
        rstd_from(c2, D, c2 + 1)
        stt(xs2v, x1[:, slot, :], stat[:, c2 + 1:c2 + 2], gpremlp[:], ALU.mult, ALU.mult,
            reads=R(("x1", slot), ("stat", c2 + 1), "gpremlp"), writes=W("mix_sb"))
        tb_ = 4 if bgm else 0
        for ch in range(8):
            tr(banks_bf[tb_][:, ch * 128:(ch + 1) * 128], xs2v[:, ch * 128:(ch + 1) * 128], ident_bf[:],
               reads=R("mix_sb", "ident_bf"), writes=W(BK(tb_)))
        cp("act", xn2T[:, :, slot * 128:(slot + 1) * 128], banks_bf[tb_][:, :].rearrange("p (c t) -> p c t", c=8),
           reads=R(BK(tb_)), writes=W(("xn2T", slot)))

    wupc = [0]
    wdnc = [0]

    def ffn(nb, out_rows_fn):
        ntok = nb * 128
        for g in range(16):
            sl = wupc[0] % NWS
            wupc[0] += 1
            dma(wup[sl][:], wup_bf[g], ["wup_bf"], [("wup", sl)])
            for f2 in range(2):
                fc = 2 * g + f2
                bk = fc % 2
                for ch in range(8):
                    mm(banks[bk][:, 0:ntok], wup[sl][:, ch, f2 * 128:(f2 + 1) * 128], xn2T[:, ch, 0:ntok], ch == 0, ch == 7,
                       reads=R(("wup", sl), *[("xn2T", j) for j in range(nb)]), writes=W(BK(bk)))
                act(rl[bk][:, 0:ntok], banks[bk][:, 0:ntok], AF.Relu, reads=R(BK(bk)), writes=W(("rl", bk)))
                tt("pool", hid[:, fc, 0:ntok], rl[bk][:, 0:ntok], rl[bk][:, 0:ntok], ALU.mult, reads=R(("rl", bk)),
                   writes=W(("hid", fc)))
        c = newstat(16)
        for dh in range(2):
            for fg in range(8):
                sl = wdnc[0] % NWS
                wdnc[0] += 1
                dma(wdn[sl][:], wdn_bf[dh, fg], ["wdn_bf"], [("wdn", sl)])
                for fl in range(4):
                    fc = fg * 4 + fl
                    for t_ in range(nb):
                        mm(banks[2 + t_][:, :], hid[:, fc, t_ * 128:(t_ + 1) * 128], wdn[sl][:, fl, :], fc == 0, fc == 31,
                           reads=R(("hid", fc), ("wdn", sl)), writes=W(BK(2 + t_)))
            for t_ in range(nb):
                cc = c + t_ * 4 + dh
                act(ff_sb[:, t_, dh * 512:(dh + 1) * 512], banks[2 + t_][:, :], AF.Square, reads=R(BK(2 + t_)),
                    writes=W(("ff_sb", t_), ("stat", cc)), accum_out=stat[:, cc:cc + 1])
                cp("dve", ff_sb[:, t_, dh * 512:(dh + 1) * 512], banks[2 + t_][:, :], reads=R(BK(2 + t_)),
                   writes=W(("ff_sb", t_)))
        for t_ in range(nb):
            cc = c + t_ * 4
            tt("dve", stat[:, cc + 2:cc + 3], stat[:, cc:cc + 1], stat[:, cc + 1:cc + 2], ALU.add,
               reads=R(("stat", cc), ("stat", cc + 1)), writes=W(("stat", cc + 2)))
            rstd_from(cc + 2, D, cc + 3)
            yt = ff_sb[:, t_, :]
            ky = ("ff_sb", t_)
            stt(yt, yt, stat[:, cc + 3:cc + 4], gpostmlp[:], ALU.mult, ALU.mult,
                reads=R(ky, ("stat", cc + 3), "gpostmlp"), writes=W(ky))
            tt("dve", yt, yt, x1[:, t_, :], ALU.add, reads=R(ky, ("x1", t_)), writes=W(ky))
            dma(out_rows_fn(t_), yt, R(ky), W())

    def record(fn_, *a_):
        prev = S.defer
        S.defer = []
        fn_(*a_)
        lst = S.defer
        S.defer = prev
        return lst

    def sample_prep(bgm):
        XLK_ = [("xl", ct) for ct in range(4)]
        HTK_ = [("hT", ct) for ct in range(4)]
        dma(st_h_sb[:], st_h, R(), W("st_h"))
        dma(xl_s[:, :, :, 0:3], st_conv, R(), W(XLK_))
        xcnt[0] = 0
        xpre[0] = False
        norm_transpose(xs_tok, 0)
        kv_block(0, NB, ckv_s, kr_s, Vn[:], KTn[:], krTn[:], "Vn", "KTn", "krTn")
        lru(128, True)
        cp("dve", hs_sb[:], hT[:, :, 0:128].rearrange("p c (s t) -> p c s t", s=4)[:, :, :, 31], reads=R(HTK_), writes=W("hs_sb"))
        dma(h_s, hs_sb[:], R("hs_sb"), W())
        dma(conv_s, xl_s[:, :, :, 32:35], R(XLK_), W())
        cp("dve", xnT_own[:], xnT[:, :, 0:128], reads=R("xnT"), writes=W("xnT_own"))
        for ct in range(4):
            cp("dve", hsel[:, ct, :], hT[:, ct, 0:128], reads=R(("hT", ct)), writes=W(("hsel", ct)))
        dma(tabo[:], tab_real[NB], R(), W("tabo"))
        owned_front(tabo, BSS[0], bgm)

    BSS = [dict(y=y_lruT, ky="y_lruT", ql=qlatT, kql="qlatT", qr=qrT, kqr="qrT"),
           dict(y=y_lruT2, ky=("wdn", 0), ql=qlatT2, kql=("wdn", 0), qr=qrT2, kqr=("wdn", 1))]

    def program():
        stage(1)
        XLK = [("xl", ct) for ct in range(4)]
        HTK = [("hT", ct) for ct in range(4)]
        VK = [("V", kb) for kb in range(NB)]
        S.op("dve", lambda e: e.memset(xl_p[:, :, 0:3], 0.0), reads=R(), writes=W(XLK))
        xp_all_v = xp_all.rearrange("(b p) d -> b p d", p=128)
        xp_own_v = xp_own.rearrange("(b p) d -> b p d", p=128)
        y_own_v = y_own.rearrange("(b p) d -> b p d", p=128)
        ckv_p_v = ckv_p.rearrange("(b p) d -> b p d", p=128)
        kr_p_v = kr_p.rearrange("(b p) d -> b p d", p=128)
        for stile in range(4):
            bg = []

            def keysets(item):
                ops_ = item if isinstance(item, list) else [item]
                r_, w_ = set(), set()
                for (_e, _f, reads_, writes_, _d, _g, _c) in ops_:
                    r_.update(reads_)
                    w_.update(writes_)
                r_.discard("PHASE")
                return r_, w_

            def safe_merge(A, B):
                if not SAFE_MERGE:
                    return A + B
                if TIMED_MERGE:
                    return sched_merge(A, B, constrained=True)
                from collections import Counter
                remR, remW = Counter(), Counter()
                ka = [keysets(a) for a in A]
                kb_ = [keysets(b) for b in B]
                for r_, w_ in ka:
                    remR.update(r_)
                    remW.update(w_)
                out = []
                ia = ib = 0
                while ia < len(A) or ib < len(B):
                    if ia < len(A):
                        out.append(A[ia])
                        remR.subtract(ka[ia][0])
                        remW.subtract(ka[ia][1])
                        ia += 1
                    if ib < len(B):
                        r_, w_ = kb_[ib]
                        conflict = any(remR[k] > 0 or remW[k] > 0 for k in w_) or any(remW[k] > 0 for k in r_)
                        if not conflict or ia >= len(A):
                            out.append(B[ib])
                            ib += 1
                return out

            def record(fn_, *a_):
                prev = S.defer
                S.defer = []
                fn_(*a_)
                lst = S.defer
                S.defer = prev
                return lst

            def phaseA_parts(p_, j_, load_x1=True, last_prefetch=True):
                parts = []
                if load_x1:
                    parts.append(record(lambda: dma(x1[:, j_, :], xp_own_v[p_], R(), W(("x1", j_)))))
                for rb in range(2):
                    b = 2 * p_ + rb
                    nxt = xp_all_v[b + 1] if (b + 1 < NB and (rb == 0 or last_prefetch)) else None

                    def one(b=b, rb=rb, nxt=nxt):
                        norm_transpose(xp_all_v[b], rb * 128, nxt)
                        kv_block(rb * 128, b, ckv_p_v[b], kr_p_v[b], V[:, b, :], KT[:, b * 128:(b + 1) * 128],
                                 krT[:, b * 128:(b + 1) * 128], ("V", b), ("KT", b), ("krT", b))
                    parts.append(record(one))

                def l_():
                    lru(256, False)
                    if p_ == NOWN - 1:
                        dma(h_p, hstate[:], R([("hstate", ct) for ct in range(4)]), W())
                        dma(conv_p, xl_p[:, :, 0:3], R(XLK), W())
                parts.append(record(l_))
                return parts

            def front(p_, BS, bgm):
                ts("dve", xnT_own[:], xnT[:, :, 0:128], sel[:, 0:1], None, ALU.mult, None, reads=R("xnT", "sel"), writes=W("xnT_own"))
                stt(xnT_own[:], xnT[:, :, 128:256], sel[:, 1:2], xnT_own[:], ALU.mult, ALU.add,
                    reads=R("xnT", "sel", "xnT_own"), writes=W("xnT_own"))
                for ct in range(4):
                    ts("dve", hsel[:, ct, :], hT[:, ct, 0:128], sel[:, 0:1], None, ALU.mult, None, reads=R(("hT", ct), "sel"),
                       writes=W(("hsel", ct)))
                    stt(hsel[:, ct, :], hT[:, ct, 128:256], sel[:, 1:2], hsel[:, ct, :], ALU.mult, ALU.add,
                        reads=R(("hT", ct), "sel", ("hsel", ct)), writes=W(("hsel", ct)))
                dma(tabo[:], tab_own[p_], R(), W("tabo"))
                owned_front(tabo, BS, bgm)

            for j in range(4):
                p = stile * 4 + j
                BS = BSS[j % 2]
                if j == 0:
                    if stile == 0 or not CROSS_STILE:
                        m_ = []
                        for part in phaseA_parts(p, j) + [record(front, p, BS, False)]:
                            m_ = safe_merge(m_, part)
                        S.flush(m_, len(m_))
                        scratch_pump(2)
                    else:
                        dma(x1[:, 0, :], xp_own_v[p], R(), W(("x1", 0)))
                kbl = []
                nkb = 2 * p + 2
                for kb in range(nkb):
                    mk = None
                    if kb >= nkb - 2:
                        mi = kb - (nkb - 2)
                        mk = [maskb[:, mi, :], maskb[:, mi, :]]
                    kbl.append(dict(KT=KT[:, kb * 128:(kb + 1) * 128], krT=krT[:, kb * 128:(kb + 1) * 128], V=V[:, kb, :],
                                    keys=[("KT", kb), ("krT", kb), ("V", kb)], masks=mk))
                groups = []
                for hh in range(2):
                    def fin_p(rv, krv, hh=hh):
                        tt("dve", olatT[:, hh * 4:(hh + 1) * 4, :].rearrange("p h t -> p (h t)"), banks[2 + hh][:, :], rv, ALU.mult,
                           reads=R(BK(2 + hh), krv), writes=W("olatT"))
                    groups.append(dict(
                        ql=BS["ql"][:, hh * 4:(hh + 1) * 4, :].rearrange("p h t -> p (h t)"),
                        qr=BS["qr"][:, hh * 4:(hh + 1) * 4, :].rearrange("p h t -> p (h t)"), N=512,
                        kql=BS["kql"], kqr=BS["kqr"],
                        O=banks[2 + hh][:, :], L=banks[4 + hh][:, :], kO=BK(2 + hh), kL=BK(4 + hh), fin=fin_p, li=hh))
                if j < 3 or (CROSS_STILE and stile < 3):
                    jn = (j + 1) % 4
                    m_ = list(bg)
                    for part in (phaseA_parts(p + 1, jn, load_x1=(j < 3), last_prefetch=(j < 2))
                                 + [record(front, p + 1, BSS[jn % 2], True)]):
                        m_ = safe_merge(m_, part)
                    bg[:] = m_
                elif SAMPLE_PREP_BG and stile == 3 and j == 3:
                    bg[:] = safe_merge(list(bg), record(sample_prep, True))
                attn(groups, kbl, bg)
                S.flush(bg, len(bg))
                scratch_pump(5 if stile == 0 else 0)
                if j < 3 and DEFER_TAIL:
                    bg.extend(record(mix_and_prep, None, j, BS, True))
                else:
                    mix_and_prep(None, j, BS, False)
                stage(100 + p)
            scratch_pump(32)
            phase_switch()
            ffn(4, lambda t_, stile=stile: y_own_v[stile * 4 + t_])
            phase_switch()
            stage(200 + stile)

        if not SAMPLE_PREP_BG:
            sample_prep(False)
        stage(3)
        def stream_prep(s):
            cv_ = cckv[s].rearrange("(b p) r -> p b r", p=128)
            ck_ = ckr[s].rearrange("(b p) r -> p b r", p=128)
            tb_ = 4 + (s % 2)
            for q8 in range(4):
                dma(V[:, q8 * 8:(q8 + 1) * 8, :], cv_[:, q8 * 8:(q8 + 1) * 8, :], R(), W([("V", q8 * 8 + j) for j in range(8)]),
                    eng="pool")
                dma(krc[:, q8 * 8:(q8 + 1) * 8, :], ck_[:, q8 * 8:(q8 + 1) * 8, :], R(), W(("krc", q8)), eng="pool")
            for q8 in range(4):
                for j in range(8):
                    kb = q8 * 8 + j
                    tr(banks_bf[0][:, j * 128:(j + 1) * 128], V[:, kb, :], ident_bf[:], reads=R(("V", kb), "ident_bf"),
                       writes=W(BK(0)))
                cp("dve", KT[:, q8 * 1024:(q8 + 1) * 1024], banks_bf[0][:, :], reads=R(BK(0)),
                   writes=W([("KT", q8 * 8 + j) for j in range(8)]))
                for j in range(8):
                    kb = q8 * 8 + j
                    tr(banks_bf[tb_][0:32, j * 128:(j + 1) * 128], krc[:, kb, :], ident_bf[:], reads=R(("krc", q8), "ident_bf"),
                       writes=W(BK(tb_)))
                cp("act", krT[:, q8 * 1024:(q8 + 1) * 1024], banks_bf[tb_][0:32, :], reads=R(BK(tb_)),
                   writes=W([("krT", q8 * 8 + j) for j in range(8)]))

        stream_prep(0)
        for s in range(4):
            kbl = []
            for kb in range(NB):
                kbl.append(dict(KT=KT[:, kb * 128:(kb + 1) * 128], krT=krT[:, kb * 128:(kb + 1) * 128], V=V[:, kb, :],
                                keys=[("KT", kb), ("krT", kb), ("V", kb)], masks=None))
            kbl.append(dict(KT=KTn[:], krT=krTn[:], V=Vn[:], keys=["KTn", "krTn", "Vn"], masks=[masks[:, s, :]]))
            bo = 2 + s % 2
            oo = 0

            def fin_s(rv, krv, s=s, bo=bo, oo=oo):
                tt("dve", olatT[:, :, s * 32:(s + 1) * 32], banks[bo][:, oo:oo + 256].rearrange("p (h t) -> p h t", h=8),
                   rv.rearrange("p (h t) -> p h t", h=8), ALU.mult, reads=R(BK(bo, oo, oo + 256), krv), writes=W("olatT"))

            grp = dict(ql=qlatT[:, :, s * 32:(s + 1) * 32], qr=qrT[:, :, s * 32:(s + 1) * 32], N=256,
                       O=banks[bo][:, oo:oo + 256], L=banks[bo + 2][:, oo:oo + 256],
                       kO=BK(bo, oo, oo + 256), kL=BK(bo + 2, oo, oo + 256), fin=fin_s, li=s % 2,
                       kql="qlatT", kqr="qrT")
            nxt_ = record(stream_prep, s + 1) if s < 3 else []
            attn([grp], kbl, nxt_, constrained=True)
            stage(40 + s)
        mix_and_prep(xs_tok, 0, BSS[0], False)
        stage(5)
        phase_switch()
        ffn(1, lambda t_: y_s)
        phase_switch()
        stage(6)

    try:
        program()
    except _Stop:
        pass

    S.finalize()
    return nc, S


def _emit(nc, S):
    import contextlib
    with contextlib.ExitStack() as es:
        esems = {e: es.enter_context(nc.semaphore(f"s_{e}")) for e in Sched.ENGS}
        dsems = {q: [es.enter_context(nc.semaphore(f"d_{q}{i}")) for i in range(n)] for q, n in S.n_dma.items()}
        block = es.enter_context(nc.Block())

        @block.tensor
        def _(pe):
            S.emit("pe", pe, esems, dsems)

        @block.scalar
        def _(a):
            S.emit("act", a, esems, dsems)

        @block.vector
        def _(v):
            S.emit("dve", v, esems, dsems)

        @block.gpsimd
        def _(g):
            S.emit("pool", g, esems, dsems)
            for q in ("pool", "scr"):
                st = S.dma_state[q]
                for i in range(S.n_dma[q]):
                    if st["cnt"][i]:
                        g.wait_ge(dsems[q][i], 16 * st["cnt"][i])

        @block.sync
        def _(sp):
            S.emit("sp", sp, esems, dsems)
            st = S.dma_state["sp"]
            for i in range(S.n_dma["sp"]):
                if st["cnt"][i]:
                    sp.wait_ge(dsems["sp"][i], 16 * st["cnt"][i])
    return nc


_CACHE = {}


def _rope_tab(pos):
    inv = 10000.0 ** (-np.arange(0, 32, 2, dtype=np.float64) / 32.0)
    ang = pos.astype(np.float64)[:, None] * inv[None, :]
    return np.concatenate([np.cos(ang), np.sin(ang)], axis=1).astype(np.float32)


def kernel(x_prompt, x_sample, cache_ckv, cache_krope, state_lru_h, state_conv,
           norm_pre_mix, norm_post_mix, norm_pre_mlp, norm_post_mlp, w_in, conv_w, conv_b,
           lru_w_a, lru_b_a, lru_w_i, lru_b_i, lru_lambda, q_norm, w_uq, kv_norm, w_ukv,
           w_out, w_up, w_down):
    f = lambda a: np.ascontiguousarray(np.asarray(a, dtype=np.float32))
    x_prompt, x_sample = f(x_prompt), f(x_sample)
    cache_ckv, cache_krope = f(cache_ckv)[0], f(cache_krope)[0]
    state_lru_h, state_conv = f(state_lru_h)[0], f(state_conv)[0]
    if "nc" not in _CACHE:
        nc, S = build_nc()
        _emit(nc, S)
        _CACHE["nc"] = nc
    nc = _CACHE["nc"]

    rep = lambda v, n=128: np.ascontiguousarray(np.broadcast_to(f(v).reshape(1, -1), (n, f(v).size)))
    vec = np.zeros((128, 4, 8), np.float32)
    cw = f(conv_w)[0]
    for k in range(4):
        vec[:, :, k] = cw[k].reshape(4, 128).T
    vec[:, :, 4] = f(conv_b)[0].reshape(4, 128).T
    vec[:, :, 5] = f(lru_b_a)[0].reshape(4, 128).T
    vec[:, :, 6] = f(lru_b_i)[0].reshape(4, 128).T
    vec[:, :, 7] = f(lru_lambda)[0].reshape(4, 128).T
    gpm = np.ascontiguousarray(f(norm_pre_mix)[0].reshape(8, 128).T)
    shared = dict(
        w_in=f(w_in)[0], w_out=f(w_out)[0], w_up=f(w_up)[0], w_down=f(w_down)[0],
        w_uq=f(w_uq)[0].reshape(256, 768), w_ukv=f(w_ukv)[0].reshape(128, 1024),
        w_a=f(lru_w_a)[0], w_i=f(lru_w_i)[0], vec=vec, gpm=gpm,
        g_post_mix=rep(norm_post_mix), g_pre_mlp=rep(norm_pre_mlp), g_post_mlp=rep(norm_post_mlp),
        g_q=rep(q_norm), g_kv=rep(kv_norm), ident=np.eye(128, dtype=np.float32),
    )
    masks = np.full((128, 4, 256), NEG, np.float32)
    for s in range(4):
        masks[s * 32:(s + 1) * 32, s, :] = 0.0
    shared["masks"] = masks
    pos_s = SEQ + np.tile(np.arange(32), 4)
    in_maps = []
    for c in range(NCORES):
        seq, par = c // 2, c % 2
        xa = x_prompt[seq]
        own_blocks = [2 * p + par for p in range(NOWN)]
        xo = xa.reshape(NB, 128, D)[own_blocks].reshape(NOWN * 128, D)
        diag = np.zeros((128, 128), np.float32)
        diag[64:, :64] = NEG
        full = np.full((128, 128), NEG, np.float32)
        vis = np.zeros((128, 128), np.float32)
        m0, m1 = (diag, full) if par == 0 else (vis, diag)
        maskb = np.stack([np.tile(m0, (1, 4)), np.tile(m1, (1, 4))], axis=1)
        sel = np.zeros((128, 2), np.float32)
        sel[:, par] = 1.0
        tab_real = np.zeros((NB + 1, 128, 32), np.float32)
        tab_real[:NB] = _rope_tab(np.arange(SEQ)).reshape(NB, 128, 32)
        tab_real[NB] = _rope_tab(pos_s)
        tab_own = tab_real[own_blocks]
        ss = slice(4 * c, 4 * c + 4)
        sth = state_lru_h[ss]
        stc = state_conv[ss]
        m = dict(shared)
        m.update(
            xp_all=np.ascontiguousarray(xa), xp_own=np.ascontiguousarray(xo),
            xs_tok=np.ascontiguousarray(x_sample[ss].reshape(128, D)),
            cckv=np.ascontiguousarray(cache_ckv[ss]), ckr=np.ascontiguousarray(cache_krope[ss]),
            st_h=np.ascontiguousarray(sth.reshape(4, 4, 128).transpose(2, 1, 0)),
            st_conv=np.ascontiguousarray(stc.reshape(4, 3, 4, 128).transpose(3, 2, 0, 1)),
            maskb=np.ascontiguousarray(maskb), sel=sel,
            tab_real=tab_real, tab_own=np.ascontiguousarray(tab_own),
        )
        in_maps.append(m)
    res = run_bass_kernel_spmd(nc, in_maps, core_ids=list(range(NCORES)))
    R_ = res.results
    y_p = np.zeros((4, SEQ, D), np.float32)
    y_s = np.zeros((32, 32, D), np.float32)
    ckv_p = np.zeros((1, 4, SEQ, 128), np.float32)
    kr_p = np.zeros((1, 4, SEQ, 32), np.float32)
    h_p = np.zeros((1, 4, 512), np.float32)
    cv_p = np.zeros((1, 4, 3, 512), np.float32)
    ckv_s = np.zeros((1, 32, 32, 128), np.float32)
    kr_s = np.zeros((1, 32, 32, 32), np.float32)
    h_s = np.zeros((1, 32, 512), np.float32)
    cv_s = np.zeros((1, 32, 3, 512), np.float32)
    for c in range(NCORES):
        seq, par = c // 2, c % 2
        r = R_[c]
        yo = np.asarray(r["y_own"]).reshape(NOWN, 128, D)
        yv = y_p[seq].reshape(NB, 128, D)
        for p in range(NOWN):
            yv[2 * p + par] = yo[p]
        ss = slice(4 * c, 4 * c + 4)
        y_s[ss] = np.asarray(r["y_s"]).reshape(4, 32, D)
        if par == 0:
            ckv_p[0, seq] = np.asarray(r["ckv_p"])
            kr_p[0, seq] = np.asarray(r["kr_p"])
            h_p[0, seq] = np.asarray(r["h_p"]).T.reshape(512)
            cv_p[0, seq] = np.asarray(r["conv_p"]).transpose(2, 1, 0).reshape(3, 512)
        ckv_s[0, ss] = np.asarray(r["ckv_s"]).reshape(4, 32, 128)
        kr_s[0, ss] = np.asarray(r["kr_s"]).reshape(4, 32, 32)
        h_s[0, ss] = np.asarray(r["h_s"]).transpose(2, 1, 0).reshape(4, 512)
        cv_s[0, ss] = np.asarray(r["conv_s"]).transpose(2, 3, 1, 0).reshape(4, 3, 512)
    return (y_p, y_s, ckv_p, kr_p, h_p, cv_p, ckv_s, kr_s, h_s, cv_s)
```
